# Optimizing a Trainium2 kernel written in Bass

```python
import math
import jax, jax.numpy as jnp
from jax import lax
import numpy as np

D_MODEL = 1024
BATCH = 1
SEQ = 16384
DEPTH = 1

D_MIX = D_MODEL
CONV_WIDTH = D_MIX // 2
ATTN_WIDTH = D_MIX - CONV_WIDTH
DIFF_HEAD_DIM = 64
N_DIFF_HEADS = ATTN_WIDTH // (2 * DIFF_HEAD_DIM)
ROT_DIM = DIFF_HEAD_DIM // 4
ROPE_THETA = 500000.0
CONV_KERNEL = 31
D_FF = 2816
Q_BLOCK = 128
N_SUBLAYERS = 3
N_MOD = 3 * N_SUBLAYERS
IN_COLS = 2 * CONV_WIDTH + 3 * ATTN_WIDTH
ALPHA = (2.0 * DEPTH) ** 0.25
BETA = (8.0 * DEPTH) ** -0.25
LN_EPS = 1e-5

kernel_name = "hybrid_conformer_diffattn_macaron_deepnorm_adaln"


def layer_norm(x, g, b):
    xf = x.astype(jnp.float32)
    mu = jnp.mean(xf, axis=-1, keepdims=True)
    var = jnp.mean(jnp.square(xf - mu), axis=-1, keepdims=True)
    y = (xf - mu) * lax.rsqrt(var + LN_EPS)
    return (y * g.astype(jnp.float32) + b.astype(jnp.float32)).astype(x.dtype)


def rms_norm(x, g):
    xf = x.astype(jnp.float32)
    y = xf * lax.rsqrt(jnp.mean(jnp.square(xf), axis=-1, keepdims=True) + LN_EPS)
    return (y * g.astype(jnp.float32)).astype(x.dtype)


def swiglu(h, w_in, w_out):
    gate, up = jnp.split(h @ w_in, 2, axis=-1)
    return (jax.nn.silu(gate) * up) @ w_out


def partial_rope(t, cos, sin):
    rot, rest = t[..., :ROT_DIM], t[..., ROT_DIM:]
    r1, r2 = rot[..., :ROT_DIM // 2], rot[..., ROT_DIM // 2:]
    cos = cos.astype(t.dtype)[None, :, None, None, :]
    sin = sin.astype(t.dtype)[None, :, None, None, :]
    rot = jnp.concatenate([r1 * cos - r2 * sin, r2 * cos + r1 * sin], axis=-1)
    return jnp.concatenate([rot, rest], axis=-1)


def conformer_conv(glu_a, glu_b, conv_w, conv_b, ln_g, ln_b):
    u = glu_a * jax.nn.sigmoid(glu_b)
    pad = (CONV_KERNEL - 1) // 2
    u = lax.conv_general_dilated(
        u, conv_w[:, None, :].astype(u.dtype), window_strides=(1,),
        padding=[(pad, pad)], dimension_numbers=("NWC", "WIO", "NWC"),
        feature_group_count=CONV_WIDTH) + conv_b
    return jax.nn.silu(layer_norm(u, ln_g, ln_b))


def diff_attention(q, k, v, lq1, lk1, lq2, lk2, subln_g, cos, sin, lam_init):
    B, S, _ = q.shape
    H, Dh = N_DIFF_HEADS, DIFF_HEAD_DIM
    q = partial_rope(q.reshape(B, S, H, 2, Dh), cos, sin)
    k = partial_rope(k.reshape(B, S, H, 2, Dh), cos, sin)
    v = v.reshape(B, S, H, 2 * Dh).transpose(0, 2, 1, 3)
    q = q.transpose(0, 2, 3, 1, 4)
    k = k.transpose(0, 2, 3, 1, 4)
    lam = (jnp.exp(jnp.sum(lq1.astype(jnp.float32) * lk1.astype(jnp.float32)))
           - jnp.exp(jnp.sum(lq2.astype(jnp.float32) * lk2.astype(jnp.float32)))
           + lam_init)
    scale = 1.0 / math.sqrt(Dh)
    n_blk = S // Q_BLOCK
    q_blocks = jnp.moveaxis(q.reshape(B, H, 2, n_blk, Q_BLOCK, Dh), 3, 0)

    def attend(qb):
        s = jnp.einsum("bhmqd,bhmkd->bhmqk", qb, k).astype(jnp.float32) * scale
        p = jax.nn.softmax(s, axis=-1)
        a = p[:, :, 0] - lam * p[:, :, 1]
        return jnp.einsum("bhqk,bhkd->bhqd", a.astype(v.dtype), v)

    out = lax.map(attend, q_blocks)
    out = out.transpose(1, 0, 3, 2, 4).reshape(B, S, H, 2 * Dh)
    out = rms_norm(out, subln_g) * (1.0 - lam_init)
    return out.reshape(B, S, H * 2 * Dh)


def hybrid_mixer(h, w_in, conv_w, conv_b, conv_ln_g, conv_ln_b,
                 lq1, lk1, lq2, lk2, subln_g, w_out, cos, sin, lam_init):
    proj = h @ w_in
    c0 = CONV_WIDTH
    glu_a, glu_b, q, k, v = jnp.split(
        proj, [c0, 2 * c0, 2 * c0 + ATTN_WIDTH, 2 * c0 + 2 * ATTN_WIDTH], axis=-1)
    conv_out = conformer_conv(glu_a, glu_b, conv_w, conv_b, conv_ln_g, conv_ln_b)
    attn_out = diff_attention(q, k, v, lq1, lk1, lq2, lk2, subln_g, cos, sin, lam_init)
    return jnp.concatenate([conv_out, attn_out], axis=-1) @ w_out


def post_norm_sublayer(x, mod, weight, f, g, b):
    shift, scale, gate = mod[:, 0, None, :], mod[:, 1, None, :], mod[:, 2, None, :]
    y = f(x * (1.0 + scale) + shift)
    return layer_norm(ALPHA * x + weight * (1.0 + gate) * y, g, b)


def setup_inputs(seed: int = 0) -> dict:
    key = jax.random.key(seed)
    ks = jax.random.split(key, 32)
    L, D = DEPTH, D_MODEL

    def nrm(k, shape, s):
        return jax.random.normal(k, shape, jnp.float32) * s

    def gain(k, shape):
        return 1.0 + nrm(k, shape, 0.02)

    return {
        "x": nrm(ks[0], (BATCH, SEQ, D), 1.0),
        "c": nrm(ks[1], (BATCH, D), 1.0),
        "w_ada": nrm(ks[2], (L, D, N_MOD * D), 0.1 * D ** -0.5),
        "b_ada": nrm(ks[3], (L, N_MOD * D), 0.01),
        "ffn1_w_in": nrm(ks[4], (L, D, 2 * D_FF), D ** -0.5),
        "ffn1_w_out": nrm(ks[5], (L, D_FF, D), BETA * D_FF ** -0.5),
        "ln1_g": gain(ks[6], (L, D)),
        "ln1_b": nrm(ks[7], (L, D), 0.02),
        "mix_w_in": nrm(ks[8], (L, D, IN_COLS), D ** -0.5),
        "conv_w": nrm(ks[9], (L, CONV_KERNEL, CONV_WIDTH), CONV_KERNEL ** -0.5),
        "conv_b": nrm(ks[10], (L, CONV_WIDTH), 0.02),
        "conv_ln_g": gain(ks[11], (L, CONV_WIDTH)),
        "conv_ln_b": nrm(ks[12], (L, CONV_WIDTH), 0.02),
        "lambda_q1": nrm(ks[13], (L, DIFF_HEAD_DIM), 0.1),
        "lambda_k1": nrm(ks[14], (L, DIFF_HEAD_DIM), 0.1),
        "lambda_q2": nrm(ks[15], (L, DIFF_HEAD_DIM), 0.1),
        "lambda_k2": nrm(ks[16], (L, DIFF_HEAD_DIM), 0.1),
        "subln_g": gain(ks[17], (L, 2 * DIFF_HEAD_DIM)),
        "mix_w_out": nrm(ks[18], (L, D_MIX, D), BETA * D_MIX ** -0.5),
        "ln2_g": gain(ks[19], (L, D)),
        "ln2_b": nrm(ks[20], (L, D), 0.02),
        "ffn2_w_in": nrm(ks[21], (L, D, 2 * D_FF), D ** -0.5),
        "ffn2_w_out": nrm(ks[22], (L, D_FF, D), BETA * D_FF ** -0.5),
        "ln3_g": gain(ks[23], (L, D)),
        "ln3_b": nrm(ks[24], (L, D), 0.02),
    }


def reference(x, c, w_ada, b_ada, ffn1_w_in, ffn1_w_out, ln1_g, ln1_b,
              mix_w_in, conv_w, conv_b, conv_ln_g, conv_ln_b,
              lambda_q1, lambda_k1, lambda_q2, lambda_k2, subln_g, mix_w_out, ln2_g, ln2_b,
              ffn2_w_in, ffn2_w_out, ln3_g, ln3_b):
    B, S, D = x.shape
    pos = jnp.arange(S, dtype=jnp.float32)
    inv_freq = ROPE_THETA ** (-jnp.arange(0, ROT_DIM, 2, dtype=jnp.float32) / ROT_DIM)
    ang = pos[:, None] * inv_freq[None, :]
    cos, sin = jnp.cos(ang), jnp.sin(ang)
    c_act = jax.nn.silu(c)

    for l in range(DEPTH):
        lam_init = 0.8 - 0.6 * math.exp(-0.3 * l)
        mod = (c_act @ w_ada[l] + b_ada[l]).reshape(B, N_SUBLAYERS, 3, D)
        x = post_norm_sublayer(
            x, mod[:, 0], 0.5,
            lambda h: swiglu(h, ffn1_w_in[l], ffn1_w_out[l]), ln1_g[l], ln1_b[l])
        x = post_norm_sublayer(
            x, mod[:, 1], 1.0,
            lambda h: hybrid_mixer(h, mix_w_in[l], conv_w[l], conv_b[l], conv_ln_g[l], conv_ln_b[l],
                                   lambda_q1[l], lambda_k1[l], lambda_q2[l], lambda_k2[l],
                                   subln_g[l], mix_w_out[l], cos, sin, lam_init),
            ln2_g[l], ln2_b[l])
        x = post_norm_sublayer(
            x, mod[:, 2], 0.5,
            lambda h: swiglu(h, ffn2_w_in[l], ffn2_w_out[l]), ln3_g[l], ln3_b[l])
    return x
```

```python
import os
from contextlib import ExitStack

import numpy as np
import concourse.bass as bass
import concourse.mybir as mybir
from concourse.bass_utils import run_bass_kernel_spmd

F32 = mybir.dt.float32
BF16 = mybir.dt.bfloat16
AF = mybir.ActivationFunctionType
ALU = mybir.AluOpType

NCORES = 8
S = 16384
D = 1024
DFF = 2816
NJ = DFF // 128
TOK = S // NCORES
BLK = 256
NBLK = S // BLK
OWNB = TOK // BLK
ALPHA = 2.0 ** 0.25
EPS = 1e-5
EPS_A = EPS / (ALPHA * ALPHA)
LAM_INIT = 0.8 - 0.6 * 1.0
SCALE = 0.125
HALO = 15
UW = TOK + 2 * HALO
VW = 130

C_C, C_BADA, C_LN = 0, 8, 80
C_CONVW, C_CONVB, C_CLNG, C_CLNB, C_HMASK, C_LAMV, C_SUBG = 128, 252, 256, 260, 264, 266, 270
NCV = 398

DEBUG = os.environ.get("MK_DEBUG", "0") == "1"
SAME_SYNC = os.environ.get("MK_SAMESYNC", "1") == "1"


class Tk:
    __slots__ = ("sem", "val", "eng")

    def __init__(self, sem, val, eng):
        self.sem, self.val, self.eng = sem, val, eng


class Buf:
    __slots__ = ("w", "r", "name")

    def __init__(self, name=""):
        self.w = None
        self.r = []
        self.name = name


class Eng:
    def __init__(self, ctx, h, name):
        self.ctx, self.h, self.name = ctx, h, name
        self.sem = ctx.es.enter_context(ctx.nc.semaphore("s_" + name))
        self.cnt = 0
        self.waited = {}
        self.pend = []

    def wait(self, tk):
        if tk is None:
            return
        if tk.eng is self and (self.name == "pe" or not SAME_SYNC):
            return
        if tk.val is None:
            raise RuntimeError("waiting on unsignalled ticket of " + tk.eng.name)
        key = id(tk.sem)
        if self.waited.get(key, 0) >= tk.val:
            return
        self.h.wait_ge(tk.sem, tk.val)
        self.waited[key] = tk.val

    def run(self, emit, reads=(), writes=(), signal=True):
        for b in reads:
            self.wait(b.w)
        for b in writes:
            self.wait(b.w)
            for t in b.r:
                self.wait(t)
        ins = emit()
        tk = Tk(self.sem, None, self)
        self.pend.append(tk)
        if signal:
            self.cnt += 1
            ins.then_inc(self.sem, 1)
            for p in self.pend:
                p.val = self.cnt
            self.pend = []
        for b in reads:
            add_read(b, tk)
        for b in writes:
            b.w = tk
            b.r = []
        return tk


def add_read(b, tk):
    if tk.eng is not None:
        b.r = [t for t in b.r if t.eng is not tk.eng]
    b.r.append(tk)


class DSem:
    def __init__(self, ctx, name):
        self.sem = ctx.es.enter_context(ctx.nc.semaphore("d_" + name))
        self.cnt = 0


class Ctx:
    pass


def dma(ctx, q, dsem, out, in_, reads=(), writes=()):
    for b in reads:
        q.wait(b.w)
    for b in writes:
        if not (b.w is not None and b.w.eng is None and b.w.sem is dsem.sem):
            q.wait(b.w)
        for t in b.r:
            q.wait(t)
    dsem.cnt += 16
    q.h.dma_start(out=out, in_=in_).then_inc(dsem.sem, 16)
    tk = Tk(dsem.sem, dsem.cnt, None)
    for b in reads:
        add_read(b, tk)
    for b in writes:
        b.w = tk
        b.r = []
    return tk


def barrier(ctx, dsems=()):
    engs = [ctx.pe, ctx.act, ctx.dve, ctx.pool, ctx.sp]
    tks = []
    for e in (ctx.pe, ctx.act, ctx.dve, ctx.pool):
        if e.pend:
            raise RuntimeError("pending tickets at barrier on " + e.name)
        if e.cnt > 0:
            tks.append(Tk(e.sem, e.cnt, e))
    for d in dsems:
        if d.cnt > 0:
            tks.append(Tk(d.sem, d.cnt, None))
    for e in engs:
        for t in tks:
            if t.eng is e:
                continue
            e.wait(t)


def build():
    nc = bass.Bass("TRN2", target_bir_lowering=False)
    ctx = Ctx()
    ctx.nc = nc
    es = ExitStack()
    ctx.es = es

    def din(name, shape, dt=F32):
        return nc.dram_tensor(name, shape, dt, kind="ExternalInput").ap()

    xin = din("xin", [NBLK, 128, 8 * BLK])
    csin = din("csin", [NBLK, 128, 2 * BLK])
    w1i_d = din("w1i", [128, 8, 2 * DFF])
    w1o_d = din("w1o", [128, NJ, D])
    w2i_d = din("w2i", [128, 8, 2 * DFF])
    w2o_d = din("w2o", [128, NJ, D])
    wkv_d = din("wkv", [128, 8, 1024])
    wqg_d = din("wqg", [128, 8, 1536])
    wmo_d = din("wmo", [128, 8, 1024])
    wada_d = din("wada", [128, 8, 9216])
    cvec_d = din("cvec", [128, NCV])
    cmat_d = din("cmat", [128, 3 * 128])
    out_d = nc.dram_tensor("out", [128, 8, TOK], F32, kind="ExternalOutput").ap()

    skind = "ExternalOutput" if DEBUG else "Internal"
    Kscr = nc.dram_tensor("kscr", [4, 128, S], BF16, kind=skind).ap()
    Vscr = nc.dram_tensor("vscr", [4, 128, 128, VW], BF16, kind=skind).ap()
    X1scr = nc.dram_tensor("x1scr", [OWNB, 128, 8 * BLK], F32, kind=skind).ap()
    X2scr = nc.dram_tensor("x2scr", [OWNB, 128, 8 * BLK], F32, kind=skind).ap()
    H2scr = nc.dram_tensor("h2scr", [128, 8, TOK + 2 * BLK], BF16, kind=skind).ap()
    W2Ib = nc.dram_tensor("w2ib", [128, 8, 2 * DFF], BF16, kind="Internal").ap()
    W2Ob = nc.dram_tensor("w2ob", [128, NJ, D], BF16, kind="Internal").ap()

    with es:
        block = es.enter_context(nc.Block())

        @block.sync
        def _(sync):
            program(ctx, locals_=dict(
                xin=xin, csin=csin, w1i_d=w1i_d, w1o_d=w1o_d, w2i_d=w2i_d, w2o_d=w2o_d, wkv_d=wkv_d,
                wqg_d=wqg_d, wmo_d=wmo_d, wada_d=wada_d, cvec_d=cvec_d, cmat_d=cmat_d, out_d=out_d,
                Kscr=Kscr, Vscr=Vscr, X1scr=X1scr, X2scr=X2scr, H2scr=H2scr, W2Ib=W2Ib, W2Ob=W2Ob))
    return nc


def program(ctx, locals_):
    nc = ctx.nc
    es = ctx.es
    g = locals_
    xin, csin = g["xin"], g["csin"]
    Kscr, Vscr, X1scr, X2scr, H2scr = g["Kscr"], g["Vscr"], g["X1scr"], g["X2scr"], g["H2scr"]
    out_d = g["out_d"]

    pe = ctx.pe = Eng(ctx, nc.tensor, "pe")
    act = ctx.act = Eng(ctx, nc.scalar, "act")
    dve = ctx.dve = Eng(ctx, nc.vector, "dve")
    pool = ctx.pool = Eng(ctx, nc.gpsimd, "pool")
    sp = ctx.sp = Eng(ctx, nc.sync, "sp")
    all_dsems = []

    def dsem(name):
        d = DSem(ctx, name)
        all_dsems.append(d)
        return d

    def sb(scope, name, shape, dt=F32):
        return scope.enter_context(nc.sbuf_tensor("t_" + name, shape, dt))

    psf = es.enter_context(nc.psum_tensor("psf", [128, 7 * 512], F32))
    psb32 = es.enter_context(nc.psum_tensor("psb", [128, 512], F32))
    psb = psb32[:].bitcast(BF16)

    def bank(i, w=512):
        return psf[:, i * 512:i * 512 + w]

    PB = [Buf("bank%d" % i) for i in range(8)]

    cvec = sb(es, "cvec", [128, NCV])
    cmatf = sb(es, "cmatf", [128, 384])
    cmb = sb(es, "cmb", [128, 384], BF16)
    modT = sb(es, "modT", [128, 72])
    dv = sb(es, "dv", [128, 80])
    cact = sb(es, "cact", [128, 8])
    lamt = sb(es, "lamt", [128, 8])
    gsub = sb(es, "gsub", [128, 128])
    B_c = Buf("consts")

    ident_b = cmb[:, 0:128]
    perm_b = cmb[:, 128:256]
    ones_b = cmb[:, 256:384]
    ones_f = cmatf[:, 256:384]

    d_c = dsem("c")
    dma(ctx, sp, d_c, cvec[:], g["cvec_d"][:, :], writes=[B_c])
    dma(ctx, sp, d_c, cmatf[:], g["cmat_d"][:, :], writes=[B_c])
    B_cmb = Buf("cmb")
    dve.run(lambda: nc.vector.tensor_copy(out=cmb[:], in_=cmatf[:]), reads=[B_c], writes=[B_cmb])

    V_SC0, V_SH0, V_GH0, V_G2, V_B2, V_GM, V_SC2, V_SH2, V_GH2 = [i * 8 for i in range(9)]

    def dvc(base, m):
        return dv[:, base + m:base + m + 1]

    def cvc(col):
        return cvec[:, col:col + 1]

    I32 = mybir.dt.int32

    def dve_rsqrt_ops(out, v, y, t, eps, scale, bufs):
        def r(fn):
            return lambda: dve.run(fn, reads=bufs, writes=bufs)
        ops = [r(lambda: V.tensor_scalar(out=out, in0=v, scalar1=scale, scalar2=eps, op0=ALU.mult, op1=ALU.add)),
               r(lambda: V.tensor_scalar(out=y.bitcast(I32), in0=out.bitcast(I32), scalar1=1, scalar2=None,
                                         op0=ALU.logical_shift_right)),
               r(lambda: V.tensor_scalar(out=y.bitcast(I32), in0=y.bitcast(I32), scalar1=-1, scalar2=0x5f3759df,
                                         op0=ALU.mult, op1=ALU.add))]
        for it in range(3):
            dst = out if it == 2 else y
            ops.append(r(lambda: V.tensor_tensor(out=t, in0=y, in1=y, op=ALU.mult)))
            ops.append(r(lambda: V.tensor_tensor(out=t, in0=t, in1=out, op=ALU.mult)))
            ops.append(r(lambda: V.tensor_scalar(out=t, in0=t, scalar1=-0.5, scalar2=1.5, op0=ALU.mult, op1=ALU.add)))
            ops.append(r(lambda dst=dst: V.tensor_tensor(out=dst, in0=y, in1=t, op=ALU.mult)))
        return ops

    def dve_rsqrt(out, v, y, t, eps, scale, bufs):
        for f in dve_rsqrt_ops(out, v, y, t, eps, scale, bufs):
            f()

    B_mod = Buf("mod")
    sc0 = ExitStack()
    GW = 2304
    wa = [sb(sc0, "wa%d" % i, [128, 8, GW], BF16) for i in range(2)]
    cactb = sb(sc0, "cactb", [128, 8], BF16)
    B_wa = [Buf("wa0"), Buf("wa1")]
    d_wa = [dsem("wa0"), dsem("wa1")]
    B_cact = Buf("cact")
    act.run(lambda: nc.scalar.activation(out=cact[:], in_=cvec[:, C_C:C_C + 8], func=AF.Silu),
            reads=[B_c], writes=[B_cact])
    dve.run(lambda: nc.vector.tensor_copy(out=cactb[:], in_=cact[:]), reads=[B_cact], writes=[B_cact])
    for gi in range(9216 // GW):
        s = gi % 2
        dma(ctx, pool, d_wa[s], wa[s][:], g["wada_d"][:, :, gi * GW:(gi + 1) * GW], writes=[B_wa[s]])
        for jj in range(GW // 128):
            j = gi * (GW // 128) + jj
            for kc in range(8):
                last = (kc == 7)
                pe.run(lambda: nc.tensor.matmul(bank(0)[:, j:j + 1], lhsT=wa[s][:, kc, jj * 128:(jj + 1) * 128],
                                                rhs=cactb[:, kc:kc + 1], start=(kc == 0), stop=last),
                       reads=[B_wa[s], B_cact], writes=[PB[0]], signal=last)
    dve.run(lambda: nc.vector.tensor_tensor(out=modT[:], in0=bank(0)[:, 0:72], in1=cvec[:, C_BADA:C_BADA + 72],
                                            op=ALU.add), reads=[PB[0], B_c], writes=[B_mod])

    def mod(s, t):
        c = (s * 3 + t) * 8
        return modT[:, c:c + 8]

    def lnv(i):
        return cvec[:, C_LN + 8 * i:C_LN + 8 * i + 8]

    B_dv = Buf("dv")
    V = nc.vector

    def dvrun(fn):
        dve.run(fn, reads=[B_mod, B_c, B_dv], writes=[B_dv])

    dvrun(lambda: V.tensor_scalar(out=dv[:, V_SC0:V_SC0 + 8], in0=mod(0, 1), scalar1=1.0, scalar2=None, op0=ALU.add))
    dvrun(lambda: V.tensor_copy(out=dv[:, V_SH0:V_SH0 + 8], in_=mod(0, 0)))
    dvrun(lambda: V.tensor_scalar(out=dv[:, V_GH0:V_GH0 + 8], in0=mod(0, 2), scalar1=1.0, scalar2=0.5 / ALPHA,
                                  op0=ALU.add, op1=ALU.mult))
    dvrun(lambda: V.tensor_scalar(out=dv[:, V_B2:V_B2 + 8], in0=mod(1, 1), scalar1=1.0, scalar2=None, op0=ALU.add))
    dvrun(lambda: V.tensor_tensor(out=dv[:, V_G2:V_G2 + 8], in0=dv[:, V_B2:V_B2 + 8], in1=lnv(0), op=ALU.mult))
    dvrun(lambda: V.tensor_tensor(out=dv[:, V_B2:V_B2 + 8], in0=dv[:, V_B2:V_B2 + 8], in1=lnv(1), op=ALU.mult))
    dvrun(lambda: V.tensor_tensor(out=dv[:, V_B2:V_B2 + 8], in0=dv[:, V_B2:V_B2 + 8], in1=mod(1, 0), op=ALU.add))
    dvrun(lambda: V.tensor_scalar(out=dv[:, V_GM:V_GM + 8], in0=mod(1, 2), scalar1=1.0, scalar2=1.0 / ALPHA,
                                  op0=ALU.add, op1=ALU.mult))
    dvrun(lambda: V.tensor_scalar(out=dv[:, V_SC2:V_SC2 + 8], in0=mod(2, 1), scalar1=1.0, scalar2=None, op0=ALU.add))
    dvrun(lambda: V.tensor_copy(out=dv[:, V_SH2:V_SH2 + 8], in_=mod(2, 0)))
    dvrun(lambda: V.tensor_scalar(out=dv[:, V_GH2:V_GH2 + 8], in0=mod(2, 2), scalar1=1.0, scalar2=0.5 / ALPHA,
                                  op0=ALU.add, op1=ALU.mult))
    dvrun(lambda: V.tensor_scalar(out=gsub[:], in0=cvec[:, C_SUBG:C_SUBG + 128], scalar1=1.0 - LAM_INIT,
                                  scalar2=None, op0=ALU.mult))
    B_lam = Buf("lam")
    dve.run(lambda: V.tensor_tensor(out=lamt[:, 0:1], in0=cvec[:, C_LAMV:C_LAMV + 1], in1=cvec[:, C_LAMV + 1:C_LAMV + 2],
                                    op=ALU.mult), reads=[B_c], writes=[B_lam])
    dve.run(lambda: V.tensor_tensor(out=lamt[:, 1:2], in0=cvec[:, C_LAMV + 2:C_LAMV + 3], in1=cvec[:, C_LAMV + 3:C_LAMV + 4],
                                    op=ALU.mult), reads=[B_c, B_lam], writes=[B_lam])
    pe.run(lambda: nc.tensor.matmul(bank(1)[:, 0:2], lhsT=ones_f, rhs=lamt[:, 0:2], start=True, stop=True),
           reads=[B_lam, B_c], writes=[PB[1]])
    act.run(lambda: nc.scalar.activation(out=lamt[:, 2:4], in_=bank(1)[:, 0:2], func=AF.Exp),
            reads=[PB[1]], writes=[B_lam])
    dve.run(lambda: V.tensor_tensor(out=lamt[:, 4:5], in0=lamt[:, 2:3], in1=lamt[:, 3:4], op=ALU.subtract),
            reads=[B_lam], writes=[B_lam])
    dve.run(lambda: V.tensor_scalar(out=lamt[:, 5:6], in0=lamt[:, 4:5], scalar1=LAM_INIT, scalar2=None, op0=ALU.add),
            reads=[B_lam], writes=[B_lam])
    LAM = lamt[:, 5:6]
    barrier(ctx, all_dsems)
    sc0.close()

    def ffn_phase(tag, nblk, x_src, wi_d, wo_d, vsc, vsh, vgh, mode, wq=None, wsrc_bufs=()):
        sc = ExitStack()
        w_in = sb(sc, tag + "wi", [128, 8, 2 * DFF], BF16)
        w_out = sb(sc, tag + "wo", [128, NJ, D], BF16)
        xb = [sb(sc, tag + "xb%d" % i, [128, 8, BLK]) for i in range(2)]
        hb = [sb(sc, tag + "hb%d" % i, [128, 8, BLK], BF16) for i in range(2)]
        actt = sb(sc, tag + "act", [128, NJ, BLK], BF16)
        sg = [sb(sc, tag + "sg%d" % i, [128, BLK]) for i in range(2)]
        rb = [sb(sc, tag + "rb%d" % i, [128, BLK], BF16) for i in range(2)]
        sq = [sb(sc, tag + "sq%d" % i, [128, BLK], BF16) for i in range(2)]
        mu = sb(sc, tag + "mu", [128, BLK])
        rstd = sb(sc, tag + "rstd", [128, BLK])
        nmr = sb(sc, tag + "nmr", [128, BLK])
        t1 = sb(sc, tag + "t1", [128, BLK])
        t2 = sb(sc, tag + "t2", [128, BLK])
        B_t1, B_t2 = Buf("t1"), Buf("t2")
        B_win, B_wout = Buf("win"), Buf("wout")
        B_xb = [[Buf("xb%d_%d" % (i, m)) for m in range(8)] for i in range(2)]
        B_hb = [[Buf("hb%d_%d" % (i, m)) for m in range(8)] for i in range(2)]
        B_act = [Buf("act%d" % j) for j in range(NJ)]
        B_sg = [Buf("sg0"), Buf("sg1")]
        B_rb = [Buf("rb0"), Buf("rb1")]
        B_sq = [Buf("sq0"), Buf("sq1")]
        B_st = Buf("stats")
        d_wi, d_wo, d_wk = dsem(tag + "wi"), dsem(tag + "wo"), dsem(tag + "wk")
        d_x = [dsem(tag + "x0"), dsem(tag + "x1")]
        d_xs = [dsem(tag + "xs0"), dsem(tag + "xs1")]
        p1 = (mode == "p1")
        if p1:
            wkv = sb(sc, "wkv", [128, 8, 1024], BF16)
            cs = [sb(sc, "cs%d" % i, [128, 2, BLK]) for i in range(2)]
            kT = sb(sc, "kT", [128, 4, BLK], BF16)
            vout = sb(sc, "vout", [128, 2, 4, VW], BF16)
            kb = sb(sc, "kb", [128, BLK], BF16)
            B_wkv, B_cs = Buf("wkv"), [Buf("cs0"), Buf("cs1")]
            B_kT, B_vout, B_kb = Buf("kT"), Buf("vout"), Buf("kb")
            d_cs = [dsem("cs0"), dsem("cs1")]
            d_k, d_v = dsem("kst"), dsem("vst")
            d_hs = [dsem("hs0"), dsem("hs1")]
            B_scr = Buf("scr")
            dve.run(lambda: V.memset(vout[:], 1.0), writes=[B_vout])
        wq = wq or pool
        for kc in range(8):
            dma(ctx, wq, d_wi, w_in[:, kc, :], wi_d[:, kc, :], reads=list(wsrc_bufs), writes=[B_win])
        for c4 in range(2):
            dma(ctx, wq, d_wo, w_out[:, c4 * 11:(c4 + 1) * 11, :], wo_d[:, c4 * 11:(c4 + 1) * 11, :],
                reads=list(wsrc_bufs), writes=[B_wout])
        if p1:
            dma(ctx, pool, d_wk, wkv[:], g["wkv_d"][:, :, :], writes=[B_wkv])

        def load_x(b):
            s = b % 2
            dma(ctx, sp, d_x[s], xb[s][:].rearrange("p a t -> p (a t)"), x_src[b], writes=B_xb[s])

        def load_cs(b):
            s = b % 2
            dma(ctx, sp, d_cs[s], cs[s][:].rearrange("p a t -> p (a t)"), csin[b], writes=[B_cs[s]])

        def modulate(b):
            s = b % 2
            for kc in range(8):
                act.run(lambda: nc.scalar.activation(out=hb[s][:, kc, :], in_=xb[s][:, kc, :], func=AF.Identity,
                                                     scale=dvc(vsc, kc), bias=dvc(vsh, kc)),
                        reads=[B_xb[s][kc], B_dv], writes=[B_hb[s][kc]])

        def kv_pieces(b):
            s = b % 2
            pieces = []

            def kpieceA(hc):
                bk = 4 + hc % 2
                def f():
                    for kc in range(8):
                        pe.run(lambda: nc.tensor.matmul(bank(bk)[:, 0:BLK], lhsT=wkv[:, kc, hc * 128:(hc + 1) * 128],
                                                        rhs=hb[s][:, kc, :], start=(kc == 0), stop=(kc == 7)),
                               reads=[B_wkv, B_hb[s][kc]], writes=[PB[bk]], signal=(kc == 7))
                    dve.run(lambda: V.tensor_tensor(out=t1[:], in0=bank(bk)[:, 0:BLK], in1=cs[s][:, 0, :], op=ALU.mult),
                            reads=[PB[bk], B_cs[s]], writes=[B_t1])
                    dve.run(lambda: V.tensor_copy(out=kb[:], in_=bank(bk)[:, 0:BLK]), reads=[PB[bk]], writes=[B_kb])
                return f

            def kpieceB(hc):
                def f():
                    pe.run(lambda: nc.tensor.matmul(bank(6)[:, 0:BLK], lhsT=perm_b, rhs=kb[:], start=True, stop=True),
                           reads=[B_kb, B_cmb], writes=[PB[6]])
                    dve.run(lambda: V.tensor_tensor(out=t2[:], in0=bank(6)[:, 0:BLK], in1=cs[s][:, 1, :], op=ALU.mult),
                            reads=[PB[6], B_cs[s]], writes=[B_t2])
                    dve.run(lambda: V.tensor_tensor(out=kT[:, hc, :], in0=t1[:], in1=t2[:], op=ALU.add),
                            reads=[B_t1, B_t2], writes=[B_kT])
                return f

            def vpiece(tt):
                bk = 4 + tt
                def f():
                    for kc in range(8):
                        pe.run(lambda: nc.tensor.matmul(bank(bk)[:, 0:512], lhsT=hb[s][:, kc, tt * 128:(tt + 1) * 128],
                                                        rhs=wkv[:, kc, 512:1024], start=(kc == 0), stop=(kc == 7)),
                               reads=[B_wkv, B_hb[s][kc]], writes=[PB[bk]], signal=(kc == 7))
                    act.run(lambda: nc.scalar.activation(out=vout[:, tt, :, 0:128],
                                                         in_=bank(bk).rearrange("p (h d) -> p h d", h=4),
                                                         func=AF.Copy), reads=[PB[bk]], writes=[B_vout])
                return f

            def store():
                dma(ctx, sp, d_k, Kscr[:, :, b * BLK:(b + 1) * BLK].rearrange("h p t -> p h t"), kT[:],
                    reads=[B_kT], writes=[B_scr])
                for tt in range(2):
                    dma(ctx, sp, d_v, Vscr[:, :, 2 * b + tt, :].rearrange("h p d -> p h d"), vout[:, tt, :, :],
                        reads=[B_vout], writes=[B_scr])

            for hc in range(4):
                pieces.append(kpieceA(hc))
                pieces.append(kpieceB(hc))
            for tt in range(2):
                pieces.append(vpiece(tt))
            pieces.append(store)
            return pieces

        def stage1(b, pieces):
            s = b % 2
            npc = len(pieces)
            if npc <= 11:
                slots = list(range(0, 22, 2))
            else:
                slots = list(range(NJ))
            assert npc <= len(slots), npc
            sched = {slots[i]: i for i in range(npc)}
            for j in range(NJ):
                ss = j % 2
                for half in range(2):
                    cb = half * DFF + j * 128
                    bk = 2 * ss + half
                    for kc in range(8):
                        pe.run(lambda: nc.tensor.matmul(bank(bk)[:, 0:BLK], lhsT=w_in[:, kc, cb:cb + 128],
                                                        rhs=hb[s][:, kc, :], start=(kc == 0), stop=(kc == 7)),
                               reads=[B_win, B_hb[s][kc]], writes=[PB[bk]], signal=(kc == 7))
                act.run(lambda: nc.scalar.activation(out=sg[ss][:], in_=bank(2 * ss)[:, 0:BLK], func=AF.Silu),
                        reads=[PB[2 * ss]], writes=[B_sg[ss]])
                dve.run(lambda: V.tensor_tensor(out=actt[:, j, :], in0=bank(2 * ss + 1)[:, 0:BLK], in1=sg[ss][:],
                                                op=ALU.mult), reads=[PB[2 * ss + 1], B_sg[ss]], writes=[B_act[j]])
                if j in sched:
                    pieces[sched[j]]()

        def stage2(b):
            s = b % 2

            def stats_mm(m):
                r2 = m % 2
                pe.run(lambda: nc.tensor.matmul(bank(6)[:, 0:BLK], lhsT=ones_b, rhs=rb[r2][:], start=(m == 0), stop=(m == 7),
                                                skip_group_check=True), reads=[B_rb[r2], B_cmb], writes=[PB[6]], signal=False)
                pe.run(lambda: nc.tensor.matmul(bank(6)[:, BLK:2 * BLK], lhsT=ones_b, rhs=sq[r2][:], start=False, stop=(m == 7),
                                                skip_group_check=True), reads=[B_sq[r2]], writes=[PB[6]], signal=True)

            for m in range(8):
                bk = 4 + (m % 2)
                for kc in range(NJ):
                    pe.run(lambda: nc.tensor.matmul(bank(bk)[:, 0:BLK], lhsT=w_out[:, kc, m * 128:(m + 1) * 128],
                                                    rhs=actt[:, kc, :], start=(kc == 0), stop=(kc == NJ - 1)),
                           reads=[B_wout, B_act[kc]], writes=[PB[bk]], signal=(kc == NJ - 1))
                if m >= 1:
                    stats_mm(m - 1)
                dve.run(lambda: V.scalar_tensor_tensor(out=xb[s][:, m, :], in0=bank(bk)[:, 0:BLK], scalar=dvc(vgh, m),
                                                       in1=xb[s][:, m, :], op0=ALU.mult, op1=ALU.add),
                        reads=[PB[bk], B_dv], writes=[B_xb[s][m]])
                r2 = m % 2
                pool.run(lambda: nc.gpsimd.tensor_copy(out=rb[r2][:], in_=xb[s][:, m, :]),
                         reads=[B_xb[s][m]], writes=[B_rb[r2]])
                pool.run(lambda: nc.gpsimd.tensor_tensor(out=sq[r2][:], in0=xb[s][:, m, :], in1=xb[s][:, m, :], op=ALU.mult),
                         reads=[B_xb[s][m]], writes=[B_sq[r2]])
            return lambda: stats_mm(7)

        def tail_pieces(b):
            s = b % 2
            inv = 1.0 / float(D)
            pcs = []

            ops = [
                lambda: dve.run(lambda: V.tensor_scalar(out=mu[:], in0=bank(6)[:, 0:BLK], scalar1=inv, scalar2=None,
                                                        op0=ALU.mult), reads=[PB[6]], writes=[B_st]),
                lambda: dve.run(lambda: V.tensor_tensor(out=nmr[:], in0=mu[:], in1=mu[:], op=ALU.mult),
                                reads=[B_st], writes=[B_st]),
                lambda: dve.run(lambda: V.scalar_tensor_tensor(out=rstd[:], in0=bank(6)[:, BLK:2 * BLK], scalar=inv, in1=nmr[:],
                                                               op0=ALU.mult, op1=ALU.subtract),
                                reads=[PB[6], B_st], writes=[B_st]),
            ]
            ops += dve_rsqrt_ops(rstd[:], rstd[:], t1[:], t2[:], EPS_A, 1.0, [B_st, B_t1, B_t2])
            ops.append(lambda: dve.run(lambda: V.scalar_tensor_tensor(out=nmr[:], in0=mu[:], scalar=-1.0, in1=rstd[:],
                                                                      op0=ALU.mult, op1=ALU.mult),
                                       reads=[B_st], writes=[B_st]))
            for i0 in range(0, len(ops), 4):
                def grp(i0=i0):
                    for f in ops[i0:i0 + 4]:
                        f()
                pcs.append(grp)

            def t_norm(h0):
                def f():
                    sl = slice(4 * h0, 4 * h0 + 4)
                    bufs = B_xb[s][4 * h0:4 * h0 + 4]
                    dve.run(lambda: V.tensor_tensor(out=xb[s][:, sl, :], in0=xb[s][:, sl, :],
                                                    in1=rstd[:].unsqueeze(1).broadcast_to([128, 4, BLK]), op=ALU.mult),
                            reads=[B_st] + bufs, writes=bufs)
                    dve.run(lambda: V.tensor_tensor(out=xb[s][:, sl, :], in0=xb[s][:, sl, :],
                                                    in1=nmr[:].unsqueeze(1).broadcast_to([128, 4, BLK]), op=ALU.add),
                            reads=[B_st] + bufs, writes=bufs)
                return f
            pcs.append(t_norm(0))
            pcs.append(t_norm(1))

            def t_aff(h0, sc_fn, bi_fn, to_hb):
                def f():
                    for m in range(4 * h0, 4 * h0 + 4):
                        if to_hb and m % 2 == 1:
                            pool.run(lambda: nc.gpsimd.tensor_scalar(out=hb[s][:, m, :], in0=xb[s][:, m, :], scalar1=sc_fn(m),
                                                                     scalar2=bi_fn(m), op0=ALU.mult, op1=ALU.add),
                                     reads=[B_xb[s][m], B_dv, B_c], writes=[B_hb[s][m]])
                        elif to_hb:
                            act.run(lambda: nc.scalar.activation(out=hb[s][:, m, :], in_=xb[s][:, m, :], func=AF.Identity,
                                                                 scale=sc_fn(m), bias=bi_fn(m)),
                                    reads=[B_xb[s][m], B_dv, B_c], writes=[B_hb[s][m]])
                        else:
                            act.run(lambda: nc.scalar.activation(out=xb[s][:, m, :], in_=xb[s][:, m, :], func=AF.Identity,
                                                                 scale=sc_fn(m), bias=bi_fn(m)),
                                    reads=[B_xb[s][m], B_dv, B_c], writes=[B_xb[s][m]])
                return f

            if p1:
                pcs.append(t_aff(0, lambda m: dvc(V_G2, m), lambda m: dvc(V_B2, m), True))
                pcs.append(t_aff(1, lambda m: dvc(V_G2, m), lambda m: dvc(V_B2, m), True))
                col = None
                if b <= OWNB:
                    col = b * BLK
                elif b == NBLK - 1:
                    col = TOK + BLK
                own = b < OWNB

                def t_store():
                    if col is not None:
                        dma(ctx, sp, d_hs[s], H2scr[:, :, col:col + BLK], hb[s][:], reads=B_hb[s], writes=[B_scr])
                    if own:
                        t_aff(0, lambda m: cvc(C_LN + m), lambda m: cvc(C_LN + 8 + m), False)()
                        t_aff(1, lambda m: cvc(C_LN + m), lambda m: cvc(C_LN + 8 + m), False)()
                        dma(ctx, sp, d_xs[s], X1scr[b], xb[s][:].rearrange("p a t -> p (a t)"), reads=B_xb[s], writes=[B_scr])
                if col is not None or own:
                    pcs.append(t_store)
                pcs.extend(kv_pieces(b))
            else:
                pcs.append(t_aff(0, lambda m: cvc(C_LN + 32 + m), lambda m: cvc(C_LN + 40 + m), False))
                pcs.append(t_aff(1, lambda m: cvc(C_LN + 32 + m), lambda m: cvc(C_LN + 40 + m), False))

                def t_out():
                    dma(ctx, sp, d_xs[s], out_d[:, :, b * BLK:(b + 1) * BLK], xb[s][:], reads=B_xb[s], writes=[B_out])
                pcs.append(t_out)
            return pcs

        load_x(0)
        if p1:
            load_cs(0)
        modulate(0)
        pend = []
        for b in range(nblk):
            stage1(b, pend)
            if b + 1 < nblk:
                load_x(b + 1)
                if p1:
                    load_cs(b + 1)
                modulate(b + 1)
            last_stats = stage2(b)
            pend = [last_stats] + tail_pieces(b)
        for f in pend:
            f()
        barrier(ctx, all_dsems)
        sc.close()

    B_out = Buf("out")
    ffn_phase("a", NBLK, xin, g["w1i_d"], g["w1o_d"], V_SC0, V_SH0, V_GH0, "p1")

    scm = ExitStack()
    qT = sb(scm, "qT", [128, 4, TOK], BF16)
    attnT = sb(scm, "attnT", [128, 4, TOK], BF16)
    convT = sb(scm, "convT", [128, 4, TOK], BF16)
    uext = sb(scm, "uext", [128, 4, UW])
    acc = sb(scm, "acc", [128, 4, TOK])
    B_qT, B_attnT, B_convT = Buf("qT"), Buf("attnT"), Buf("convT")
    B_u = [Buf("u%d" % c) for c in range(4)]
    B_acc = [Buf("acc%d" % c) for c in range(4)]

    sc1 = ExitStack()
    wqg = sb(sc1, "wqg", [128, 8, 1536], BF16)
    h2t = [sb(sc1, "h2t%d" % i, [128, 8, 512], BF16) for i in range(2)]
    hal = sb(sc1, "hal", [128, 8, 2 * HALO], BF16)
    csq = [sb(sc1, "csq%d" % i, [128, 2, 512]) for i in range(2)]
    sgl = sb(sc1, "sgl", [128, 512])
    q1 = sb(sc1, "q1", [128, 512])
    q1p = [q1, sb(sc1, "q1b", [128, 512])]
    qbp = [sb(sc1, "qbp%d" % i, [128, 512], BF16) for i in range(2)]
    B_q1p = [Buf("q1p0"), Buf("q1p1")]
    B_qbp = [Buf("qbp0"), Buf("qbp1")]
    q2 = sb(sc1, "q2", [128, 512])
    qb_ = sb(sc1, "qb", [128, 512], BF16)
    B_wqg, B_h2t, B_hal = Buf("wqg"), [Buf("h2t0"), Buf("h2t1")], Buf("hal")
    B_csq = [Buf("csq0"), Buf("csq1")]
    B_sgl, B_q1, B_q2, B_qb = Buf("sgl"), Buf("q1"), Buf("q2"), Buf("qb")
    d_wq, d_hal = dsem("wqg"), dsem("hal")
    d_h2 = [dsem("h2t0"), dsem("h2t1")]
    d_csq = [dsem("csq0"), dsem("csq1")]
    dma(ctx, pool, d_wq, wqg[:], g["wqg_d"][:, :, :], writes=[B_wqg])
    dma(ctx, sp, d_hal, hal[:, :, 0:HALO], H2scr[:, :, TOK + 2 * BLK - HALO:TOK + 2 * BLK], writes=[B_hal])
    dma(ctx, sp, d_hal, hal[:, :, HALO:2 * HALO], H2scr[:, :, TOK:TOK + HALO], writes=[B_hal])

    def glu_chunk(cc, rhs_fn, n, ucol, rbufs, bk, hm=None):
        for half in range(2):
            for kc in range(8):
                pe.run(lambda: nc.tensor.matmul(bank(bk + half)[:, 0:n], lhsT=wqg[:, kc, half * 512 + cc * 128:half * 512 + (cc + 1) * 128],
                                                rhs=rhs_fn(kc), start=(kc == 0), stop=(kc == 7)),
                       reads=[B_wqg] + rbufs, writes=[PB[bk + half]], signal=(kc == 7))
        act.run(lambda: nc.scalar.activation(out=sgl[:, 0:n], in_=bank(bk + 1)[:, 0:n], func=AF.Sigmoid),
                reads=[PB[bk + 1]], writes=[B_sgl])
        dve.run(lambda: V.tensor_tensor(out=uext[:, cc, ucol:ucol + n], in0=bank(bk)[:, 0:n], in1=sgl[:, 0:n], op=ALU.mult),
                reads=[PB[bk], B_sgl], writes=[B_u[cc]])
        if hm is not None:
            dve.run(lambda: V.tensor_scalar(out=uext[:, cc, ucol:ucol + n], in0=uext[:, cc, ucol:ucol + n],
                                            scalar1=cvc(C_HMASK + hm), scalar2=None, op0=ALU.mult),
                    reads=[B_c, B_u[cc]], writes=[B_u[cc]])

    for cc in range(4):
        glu_chunk(cc, lambda kc: hal[:, kc, 0:HALO], HALO, 0, [B_hal], 0, hm=0)
        glu_chunk(cc, lambda kc: hal[:, kc, HALO:2 * HALO], HALO, HALO + TOK, [B_hal], 2, hm=1)
    for tb in range(4):
        s = tb % 2
        dma(ctx, sp, d_h2[s], h2t[s][:], H2scr[:, :, tb * 512:(tb + 1) * 512], writes=[B_h2t[s]])
        for i in range(2):
            dma(ctx, sp, d_csq[s], csq[s][:, :, i * BLK:(i + 1) * BLK],
                csin[2 * tb + i].rearrange("p (a t) -> p a t", a=2), writes=[B_csq[s]])
        for cc in range(4):
            glu_chunk(cc, lambda kc: h2t[s][:, kc, :], 512, HALO + tb * 512, [B_h2t[s]], 2 * (cc % 2))
        def q_proj(hc):
            bk = 4 if hc % 2 == 0 else 6
            p = hc % 2
            for kc in range(8):
                pe.run(lambda: nc.tensor.matmul(bank(bk)[:, 0:512], lhsT=wqg[:, kc, 1024 + hc * 128:1024 + (hc + 1) * 128],
                                                rhs=h2t[s][:, kc, :], start=(kc == 0), stop=(kc == 7)),
                       reads=[B_wqg, B_h2t[s]], writes=[PB[bk]], signal=(kc == 7))
            dve.run(lambda: V.tensor_tensor(out=q1p[p][:], in0=bank(bk)[:, 0:512], in1=csq[s][:, 0, :], op=ALU.mult),
                    reads=[PB[bk], B_csq[s]], writes=[B_q1p[p]])
            dve.run(lambda: V.tensor_copy(out=qbp[p][:], in_=bank(bk)[:, 0:512]), reads=[PB[bk]], writes=[B_qbp[p]])

        def q_rot(hc):
            p = hc % 2
            pe.run(lambda: nc.tensor.matmul(bank(5)[:, 0:512], lhsT=perm_b, rhs=qbp[p][:], start=True, stop=True),
                   reads=[B_qbp[p], B_cmb], writes=[PB[5]])
            dve.run(lambda: V.tensor_tensor(out=q2[:], in0=bank(5)[:, 0:512], in1=csq[s][:, 1, :], op=ALU.mult),
                    reads=[PB[5], B_csq[s]], writes=[B_q2])
            dve.run(lambda: V.tensor_tensor(out=qT[:, hc, tb * 512:(tb + 1) * 512], in0=q1p[p][:], in1=q2[:], op=ALU.add),
                    reads=[B_q1p[p], B_q2], writes=[B_qT])

        q_proj(0)
        for hc in range(4):
            if hc + 1 < 4:
                q_proj(hc + 1)
            q_rot(hc)
    barrier(ctx, all_dsems)
    sc1.close()

    def conv_ops():
        ops = []
        for cc in range(4):
            def first(cc=cc):
                dve.run(lambda: V.tensor_scalar(out=acc[:, cc, :], in0=uext[:, cc, 0:TOK], scalar1=cvc(C_CONVW + cc * 31),
                                                scalar2=cvc(C_CONVB + cc), op0=ALU.mult, op1=ALU.add),
                        reads=[B_u[cc], B_c], writes=[B_acc[cc]])
            ops.append(first)
            for tap in range(1, 31):
                def nxt(cc=cc, tap=tap):
                    dve.run(lambda: V.scalar_tensor_tensor(out=acc[:, cc, :], in0=uext[:, cc, tap:tap + TOK],
                                                           scalar=cvc(C_CONVW + cc * 31 + tap), in1=acc[:, cc, :],
                                                           op0=ALU.mult, op1=ALU.add),
                            reads=[B_u[cc], B_c, B_acc[cc]], writes=[B_acc[cc]])
                ops.append(nxt)
        return ops

    cops = conv_ops()

    sc2 = ExitStack()
    Kq = [sb(sc2, "Kq%d" % i, [128, 4096], BF16) for i in range(4)]
    Vq = [sb(sc2, "Vq%d" % i, [128, 32, VW], BF16) for i in range(4)]
    PT = [sb(sc2, "PT%d" % i, [128, 1024], BF16) for i in range(3)]
    osb = sb(sc2, "osb", [128, 3, 512])
    rz = sb(sc2, "rz", [128, 8])
    tt1 = sb(sc2, "tt1", [128, 128])
    junk = sb(sc2, "junk", [128, 128])
    ssq = sb(sc2, "ssq", [128, 4])
    ssq2 = sb(sc2, "ssq2", [128, 8])
    ssq3 = sb(sc2, "ssq3", [128, 4])
    B_ssq2 = Buf("ssq2")
    ao = sb(sc2, "ao", [128, 128], BF16)
    B_Kq = [Buf("Kq%d" % i) for i in range(4)]
    B_Vq = [Buf("Vq%d" % i) for i in range(4)]
    B_PT = [Buf("PT%d" % i) for i in range(3)]
    B_osb, B_rz, B_dd, B_tt1, B_junk, B_ssq, B_ao = [Buf(n) for n in ("osb", "rz", "dd", "tt1", "junk", "ssq", "ao")]
    d_K = [dsem("K%d" % i) for i in range(4)]
    d_V = [dsem("V%d" % i) for i in range(4)]
    B_PS = [Buf("pss0"), Buf("pss1")]
    B_PO = Buf("pso")
    B_POb = [Buf("pso%d" % i) for i in range(3)]
    B_PTR = Buf("psb")

    def pss(slot):
        return psf[:, slot * 1024:(slot + 1) * 1024]

    def pso(comp, qq):
        r = comp * 4 + qq
        return psf[:, (4 + r // 3) * 512 + (r % 3) * 129:(4 + r // 3) * 512 + (r % 3) * 129 + 129]

    def load_kv(h, qi):
        dma(ctx, sp, d_K[qi], Kq[qi][:], Kscr[h, :, qi * 4096:(qi + 1) * 4096], writes=[B_Kq[qi]])
        dma(ctx, sp, d_V[qi], Vq[qi][:], Vscr[h, :, qi * 32:(qi + 1) * 32, :], writes=[B_Vq[qi]])

    for qi in range(4):
        load_kv(0, qi)
    steps = [(h, qb, kc) for h in range(4) for qb in range(4) for kc in range(128)]

    def emit_S(t):
        h, qb, kc = steps[t]
        qi, kk = kc // 32, kc % 32
        sl = t % 2
        for comp in range(2):
            pe.run(lambda: nc.tensor.matmul(pss(sl)[:, comp * 512:(comp + 1) * 512],
                                            lhsT=Kq[qi][comp * 64:(comp + 1) * 64, kk * 128:(kk + 1) * 128],
                                            rhs=qT[comp * 64:(comp + 1) * 64, h, qb * 512:(qb + 1) * 512],
                                            start=True, stop=True),
                   reads=[B_Kq[qi], B_qT], writes=[B_PS[sl]], signal=(comp == 1))

    def emit_exp(t):
        sl, pl = t % 2, t % 3
        act.run(lambda: nc.scalar.activation(out=PT[pl][:], in_=pss(sl), func=AF.Exp, scale=SCALE),
                reads=[B_PS[sl]], writes=[B_PT[pl]])

    def emit_AV(t):
        h, qb, kc = steps[t]
        qi, kk = kc // 32, kc % 32
        pl = t % 3
        for comp in range(2):
            for qq in range(4):
                r = comp * 4 + qq
                pe.run(lambda: nc.tensor.matmul(pso(comp, qq), lhsT=PT[pl][:, comp * 512 + qq * 128:comp * 512 + (qq + 1) * 128],
                                                rhs=Vq[qi][:, kk, 0:129], start=(kc == 0 and r % 3 == 0),
                                                stop=(kc == 127), skip_group_check=True),
                       reads=[B_PT[pl], B_Vq[qi]], writes=[B_POb[r // 3]], signal=(r == 7))

    dd4 = sb(sc2, "dd4", [128, 4, 128])
    epi = []
    epi_t0 = 0

    def ocol(comp, qq, a, b):
        r = comp * 4 + qq
        return osb[:, r // 3, (r % 3) * 129 + a:(r % 3) * 129 + b]

    def epilogue_pieces(h, qb):
        pcs = []

        def e_rz():
            for comp in range(2):
                for qq in range(4):
                    dve.run(lambda: V.reciprocal(out=rz[:, comp * 4 + qq:comp * 4 + qq + 1], in_=ocol(comp, qq, 128, 129)),
                            reads=[B_osb], writes=[B_rz])
            dve.run(lambda: V.tensor_scalar(out=rz[:, 4:8], in0=rz[:, 4:8], scalar1=LAM, scalar2=None, op0=ALU.mult),
                    reads=[B_rz, B_lam], writes=[B_rz])
        pcs.append(e_rz)

        def e_dd(qq):
            def f():
                dve.run(lambda: V.tensor_scalar(out=tt1[:], in0=ocol(1, qq, 0, 128), scalar1=rz[:, 4 + qq:5 + qq], scalar2=None,
                                                op0=ALU.mult), reads=[B_osb, B_rz], writes=[B_tt1])
                dve.run(lambda: V.scalar_tensor_tensor(out=dd4[:, qq, :], in0=ocol(0, qq, 0, 128), scalar=rz[:, qq:qq + 1], in1=tt1[:],
                                                       op0=ALU.mult, op1=ALU.subtract), reads=[B_osb, B_rz, B_tt1], writes=[B_dd])
                dve.run(lambda: V.scalar_tensor_tensor(out=junk[:], in0=dd4[:, qq, :], scalar=1.0, in1=dd4[:, qq, :], op0=ALU.mult,
                                                       op1=ALU.mult, accum_out=ssq[:, qq:qq + 1]), reads=[B_dd], writes=[B_junk, B_ssq])
            return f
        for qq in range(4):
            pcs.append(e_dd(qq))

        def e_sqrt():
            dve_rsqrt(ssq2[:, 4:8], ssq[:, 0:4], ssq2[:, 0:4], ssq3[:, 0:4], EPS, 1.0 / 128.0, [B_ssq, B_ssq2])
        pcs.append(e_sqrt)

        def e_out(qq):
            def f():
                dve.run(lambda: V.scalar_tensor_tensor(out=ao[:], in0=dd4[:, qq, :], scalar=ssq2[:, 4 + qq:5 + qq], in1=gsub[:],
                                                       op0=ALU.mult, op1=ALU.mult), reads=[B_dd, B_ssq2, B_dv], writes=[B_ao])
                pe.run(lambda: nc.tensor.matmul(psb[:, qq * 128:(qq + 1) * 128], lhsT=ao[:], rhs=ident_b, is_transpose=True,
                                                start=True, stop=True), reads=[B_ao, B_cmb], writes=[B_PTR])
                dve.run(lambda: V.tensor_copy(out=attnT[:, h, qb * 512 + qq * 128:qb * 512 + (qq + 1) * 128],
                                              in_=psb[:, qq * 128:(qq + 1) * 128]), reads=[B_PTR], writes=[B_attnT])
            return f
        for qq in range(4):
            pcs.append(e_out(qq))
        return pcs

    B_w2pc = Buf("w2pc")
    d_pc = dsem("w2pc")
    for kc in range(8):
        dma(ctx, pool, d_pc, g["W2Ib"][:, kc, :], g["w2i_d"][:, kc, :], writes=[B_w2pc])
    for c4 in range(2):
        dma(ctx, pool, d_pc, g["W2Ob"][:, c4 * 11:(c4 + 1) * 11, :], g["w2o_d"][:, c4 * 11:(c4 + 1) * 11, :], writes=[B_w2pc])

    emit_S(0)
    emit_exp(0)
    emit_S(1)
    emit_exp(1)
    for t in range(len(steps)):
        h, qb, kc = steps[t]
        qi, kk = kc // 32, kc % 32
        if t + 2 < len(steps):
            emit_S(t + 2)
        emit_AV(t)
        if t + 2 < len(steps):
            emit_exp(t + 2)
        if cops and t % 3 == 0:
            cops.pop(0)()
        if qb == 3 and h < 3 and kk == 31:
            load_kv(h + 1, qi)
        if epi and (t - epi_t0) % 2 == 1:
            epi.pop(0)()
        if kc == 127:
            while epi:
                epi.pop(0)()
            for bk in range(3):
                dve.run(lambda: V.tensor_copy(out=osb[:, bk, :], in_=psf[:, (4 + bk) * 512:(5 + bk) * 512]),
                        reads=[B_POb[bk]], writes=[B_osb])
            epi.extend(epilogue_pieces(h, qb))
            epi_t0 = t
    while epi:
        epi.pop(0)()
    while cops:
        cops.pop(0)()
    barrier(ctx, all_dsems)
    sc2.close()

    sc4 = ExitStack()
    wmo = sb(sc4, "wmo", [128, 8, 1024], BF16)
    x1t = [sb(sc4, "x1t%d" % i, [128, 8, 512]) for i in range(2)]
    mu4 = sb(sc4, "mu4", [128, 512])
    rs4 = sb(sc4, "rs4", [128, 512])
    nm4 = sb(sc4, "nm4", [128, 512])
    B_wmo = Buf("wmo")
    B_x1t = [[Buf("x1t%d_%d" % (i, m)) for m in range(8)] for i in range(2)]
    B_st4 = Buf("st4")
    d_wm = dsem("wmo")
    d_x1 = [dsem("x1t0"), dsem("x1t1")]
    d_x2 = [dsem("x2s0"), dsem("x2s1")]
    B_scr2 = Buf("scr2")
    dma(ctx, pool, d_wm, wmo[:], g["wmo_d"][:, :, :], writes=[B_wmo])
    SB = [(5, 6), (2, 3)]

    rbb = [sb(sc4, "rbb%d" % i, [128, 512], BF16) for i in range(4)]
    sqb = [sb(sc4, "sqb%d" % i, [128, 512], BF16) for i in range(4)]
    B_rbb = [Buf("rbb%d" % i) for i in range(4)]
    B_sqb = [Buf("sqb%d" % i) for i in range(4)]
    stat_ctr = [0]

    def stat_mm(par, src, bufs, m, nch):
        b1, b2 = SB[par]
        r2 = stat_ctr[0] % 4
        stat_ctr[0] += 1
        act.run(lambda: nc.scalar.activation(out=rbb[r2][:], in_=src, func=AF.Copy), reads=bufs, writes=[B_rbb[r2]])
        act.run(lambda: nc.scalar.activation(out=sqb[r2][:], in_=src, func=AF.Square), reads=bufs, writes=[B_sqb[r2]])
        pe.run(lambda: nc.tensor.matmul(bank(b1), lhsT=ones_b, rhs=rbb[r2][:], start=(m == 0), stop=(m == nch - 1)),
               reads=[B_rbb[r2], B_cmb], writes=[PB[b1]], signal=(m == nch - 1))
        pe.run(lambda: nc.tensor.matmul(bank(b2), lhsT=ones_b, rhs=sqb[r2][:], start=(m == 0), stop=(m == nch - 1)),
               reads=[B_sqb[r2], B_cmb], writes=[PB[b2]], signal=True)

    def lnstats512(par, nfeat, eps):
        b1, b2 = SB[par]
        inv = 1.0 / nfeat
        dve.run(lambda: V.tensor_scalar(out=mu4[:], in0=bank(b1), scalar1=inv, scalar2=None, op0=ALU.mult),
                reads=[PB[b1]], writes=[B_st4])
        dve.run(lambda: V.tensor_tensor(out=nm4[:], in0=mu4[:], in1=mu4[:], op=ALU.mult), reads=[B_st4], writes=[B_st4])
        dve.run(lambda: V.scalar_tensor_tensor(out=rs4[:], in0=bank(b2), scalar=inv, in1=nm4[:], op0=ALU.mult, op1=ALU.subtract),
                reads=[PB[b2], B_st4], writes=[B_st4])
        act.run(lambda: nc.scalar.activation(out=rs4[:], in_=rs4[:], func=AF.Sqrt, bias=eps, scale=1.0),
                reads=[B_st4], writes=[B_st4])
        dve.run(lambda: V.reciprocal(out=rs4[:], in_=rs4[:]), reads=[B_st4], writes=[B_st4])
        dve.run(lambda: V.scalar_tensor_tensor(out=nm4[:], in0=mu4[:], scalar=-1.0, in1=rs4[:], op0=ALU.mult, op1=ALU.mult),
                reads=[B_st4], writes=[B_st4])

    B_acct = [[Buf("acct%d_%d" % (tb, cc)) for cc in range(4)] for tb in range(4)]

    def cl_A(tb):
        tsl = slice(tb * 512, (tb + 1) * 512)
        for cc in range(4):
            stat_mm(tb % 2, acc[:, cc, tsl], [B_acc[cc], B_acct[tb][cc]], cc, 4)

    def cl_B(tb):
        tsl = slice(tb * 512, (tb + 1) * 512)
        lnstats512(tb % 2, 512.0, EPS)
        dve.run(lambda: V.tensor_tensor(out=acc[:, :, tsl], in0=acc[:, :, tsl],
                                        in1=rs4[:].unsqueeze(1).broadcast_to([128, 4, 512]), op=ALU.mult),
                reads=[B_st4] + B_acct[tb], writes=B_acct[tb])
        dve.run(lambda: V.tensor_tensor(out=acc[:, :, tsl], in0=acc[:, :, tsl],
                                        in1=nm4[:].unsqueeze(1).broadcast_to([128, 4, 512]), op=ALU.add),
                reads=[B_st4] + B_acct[tb], writes=B_acct[tb])
        for cc in range(4):
            act.run(lambda: nc.scalar.activation(out=convT[:, cc, tsl], in_=acc[:, cc, tsl], func=AF.Silu,
                                                 scale=cvc(C_CLNG + cc), bias=cvc(C_CLNB + cc)),
                    reads=[B_acct[tb][cc], B_c], writes=[B_convT])

    cl_A(0)
    cl_A(1)
    cl_B(0)
    cl_A(2)
    cl_B(1)
    cl_A(3)
    cl_B(2)
    cl_B(3)

    def p4_tail(tb):
        s = tb % 2
        par = tb % 2
        pcs = []
        pcs.append(lambda: stat_mm(par, x1t[s][:, 7, :], [B_x1t[s][7]], 7, 8))
        pcs.append(lambda: lnstats512(par, float(D), EPS_A))

        def nrm(h0):
            def f():
                sl = slice(4 * h0, 4 * h0 + 4)
                bufs = B_x1t[s][4 * h0:4 * h0 + 4]
                dve.run(lambda: V.tensor_tensor(out=x1t[s][:, sl, :], in0=x1t[s][:, sl, :],
                                                in1=rs4[:].unsqueeze(1).broadcast_to([128, 4, 512]), op=ALU.mult),
                        reads=[B_st4] + bufs, writes=bufs)
                dve.run(lambda: V.tensor_tensor(out=x1t[s][:, sl, :], in0=x1t[s][:, sl, :],
                                                in1=nm4[:].unsqueeze(1).broadcast_to([128, 4, 512]), op=ALU.add),
                        reads=[B_st4] + bufs, writes=bufs)
                for m in range(4 * h0, 4 * h0 + 4):
                    act.run(lambda: nc.scalar.activation(out=x1t[s][:, m, :], in_=x1t[s][:, m, :], func=AF.Identity,
                                                         scale=cvc(C_LN + 16 + m), bias=cvc(C_LN + 24 + m)),
                            reads=[B_x1t[s][m], B_c], writes=[B_x1t[s][m]])
            return f
        pcs.append(nrm(0))
        pcs.append(nrm(1))

        def st():
            for i in range(2):
                dma(ctx, sp, d_x2[s], X2scr[2 * tb + i].rearrange("p (a t) -> p a t", a=8), x1t[s][:, :, i * BLK:(i + 1) * BLK],
                    reads=B_x1t[s], writes=[B_scr2])
        pcs.append(st)
        return pcs

    pend4 = []
    for tb in range(4):
        s = tb % 2
        par = tb % 2
        tsl = slice(tb * 512, (tb + 1) * 512)
        for m in range(8):
            bk = m % 2
            if m == 0:
                for i in range(2):
                    dma(ctx, sp, d_x1[s], x1t[s][:, :, i * BLK:(i + 1) * BLK],
                        X1scr[2 * tb + i].rearrange("p (a t) -> p a t", a=8), writes=B_x1t[s])
            for kc in range(8):
                rhs = convT[:, kc, tsl] if kc < 4 else attnT[:, kc - 4, tsl]
                pe.run(lambda: nc.tensor.matmul(bank(bk), lhsT=wmo[:, kc, m * 128:(m + 1) * 128], rhs=rhs,
                                                start=(kc == 0), stop=(kc == 7)),
                       reads=[B_wmo, B_convT, B_attnT], writes=[PB[bk]], signal=(kc == 7))
            if pend4:
                pend4.pop(0)()
            if m >= 1:
                stat_mm(par, x1t[s][:, m - 1, :], [B_x1t[s][m - 1]], m - 1, 8)
            dve.run(lambda: V.scalar_tensor_tensor(out=x1t[s][:, m, :], in0=bank(bk), scalar=dvc(V_GM, m), in1=x1t[s][:, m, :],
                                                   op0=ALU.mult, op1=ALU.add), reads=[PB[bk], B_dv], writes=[B_x1t[s][m]])
        while pend4:
            pend4.pop(0)()
        pend4 = p4_tail(tb)
    while pend4:
        pend4.pop(0)()
    barrier(ctx, all_dsems)
    sc4.close()
    scm.close()

    ffn_phase("b", OWNB, X2scr, g["W2Ib"], g["W2Ob"], V_SC2, V_SH2, V_GH2, "p5", wq=sp, wsrc_bufs=[B_w2pc])


def _pm(w, nk):
    return np.ascontiguousarray(w.reshape(nk, 128, -1).transpose(1, 0, 2))


def _fm(v, n):
    return np.ascontiguousarray(np.asarray(v, np.float32).reshape(n, 128).T)


_NC_CACHE = {}


def kernel(**inp):
    x = np.asarray(inp["x"], np.float32)[0]
    c = np.asarray(inp["c"], np.float32)[0]
    f = lambda k: np.asarray(inp[k], np.float32)[0]
    w_ada, b_ada = f("w_ada"), f("b_ada")
    mix_w_in = f("mix_w_in")

    shared = {
        "w1i": _pm(f("ffn1_w_in"), 8), "w1o": _pm(f("ffn1_w_out"), NJ),
        "w2i": _pm(f("ffn2_w_in"), 8), "w2o": _pm(f("ffn2_w_out"), NJ),
        "wkv": _pm(mix_w_in[:, 1536:2560], 8), "wqg": _pm(mix_w_in[:, 0:1536], 8),
        "wmo": _pm(f("mix_w_out"), 8), "wada": _pm(w_ada, 8),
    }
    cv = np.zeros((128, NCV), np.float32)
    cv[:, C_C:C_C + 8] = _fm(c, 8)
    cv[:, C_BADA:C_BADA + 72] = _fm(b_ada, 72)
    for i, k in enumerate(("ln1_g", "ln1_b", "ln2_g", "ln2_b", "ln3_g", "ln3_b")):
        cv[:, C_LN + 8 * i:C_LN + 8 * i + 8] = _fm(f(k), 8)
    cw = f("conv_w")
    cv[:, C_CONVW:C_CONVW + 124] = cw.T.reshape(4, 128, 31).transpose(1, 0, 2).reshape(128, 124)
    cv[:, C_CONVB:C_CONVB + 4] = _fm(f("conv_b"), 4)
    cv[:, C_CLNG:C_CLNG + 4] = _fm(f("conv_ln_g"), 4)
    cv[:, C_CLNB:C_CLNB + 4] = _fm(f("conv_ln_b"), 4)
    for i, k in enumerate(("lambda_q1", "lambda_k1", "lambda_q2", "lambda_k2")):
        cv[0:64, C_LAMV + i] = f(k)
    cv[:, C_SUBG:C_SUBG + 128] = f("subln_g")[None, :]

    cm = np.zeros((128, 3, 128), np.float32)
    cm[:, 0, :] = np.eye(128, dtype=np.float32)
    for m in range(128):
        d = m % 64
        if d < 8:
            cm[m + 8, 1, m] = 1.0
        elif d < 16:
            cm[m - 8, 1, m] = 1.0
    cm[:, 2, :] = 1.0
    cm = cm.reshape(128, 384)

    pos = np.arange(S, dtype=np.float32)
    inv_freq = (500000.0 ** (-np.arange(0, 16, 2, dtype=np.float32) / np.float32(16))).astype(np.float32)
    ang = (pos[:, None] * inv_freq[None, :]).astype(np.float32)
    cos, sin = np.cos(ang).astype(np.float32), np.sin(ang).astype(np.float32)
    Ct = np.ones((128, S), np.float32)
    St = np.zeros((128, S), np.float32)
    for comp in range(2):
        b0 = comp * 64
        Ct[b0:b0 + 8] = cos.T
        Ct[b0 + 8:b0 + 16] = cos.T
        St[b0:b0 + 8] = -sin.T
        St[b0 + 8:b0 + 16] = sin.T

    xT = np.ascontiguousarray(x.T)
    in_maps = []
    for r in range(NCORES):
        idx = (np.arange(S) + r * TOK) % S
        xr = xT[:, idx].reshape(8, 128, NBLK, BLK).transpose(2, 1, 0, 3).reshape(NBLK, 128, 8 * BLK)
        csr = np.stack([Ct[:, idx], St[:, idx]], 1).reshape(128, 2, NBLK, BLK).transpose(2, 0, 1, 3).reshape(NBLK, 128, 2 * BLK)
        cvr = cv.copy()
        cvr[:, C_HMASK] = 1.0 if r > 0 else 0.0
        cvr[:, C_HMASK + 1] = 1.0 if r < NCORES - 1 else 0.0
        m = dict(shared)
        m.update({"xin": np.ascontiguousarray(xr), "csin": np.ascontiguousarray(csr), "cvec": cvr, "cmat": cm})
        in_maps.append(m)

    if "nc" not in _NC_CACHE:
        _NC_CACHE["nc"] = build()
    res = run_bass_kernel_spmd(_NC_CACHE["nc"], in_maps, core_ids=list(range(NCORES)))
    out = np.empty((1, S, D), np.float32)
    for r in range(NCORES):
        o = np.asarray(res.results[r]["out"])
        out[0, r * TOK:(r + 1) * TOK, :] = o.transpose(2, 1, 0).reshape(TOK, D)
    if DEBUG:
        kernel.debug = res.results
    return out
```

```python
import os
from contextlib import ExitStack

import numpy as np
import concourse.bass as bass
import concourse.mybir as mybir
from concourse.bass_utils import run_bass_kernel_spmd

F32 = mybir.dt.float32
BF16 = mybir.dt.bfloat16
AF = mybir.ActivationFunctionType
ALU = mybir.AluOpType

NCORES = 8
S = 16384
D = 1024
DFF = 2816
NJ = DFF // 128
TOK = S // NCORES
BLK = 256
NBLK = S // BLK
OWNB = TOK // BLK
ALPHA = 2.0 ** 0.25
EPS = 1e-5
EPS_A = EPS / (ALPHA * ALPHA)
LAM_INIT = 0.8 - 0.6 * 1.0
SCALE = 0.125
HALO = 15
UW = TOK + 2 * HALO
VW = 130

C_C, C_BADA, C_LN = 0, 8, 80
C_CONVW, C_CONVB, C_CLNG, C_CLNB, C_HMASK, C_LAMV, C_SUBG = 128, 252, 256, 260, 264, 266, 270
NCV = 398

DEBUG = os.environ.get("MK_DEBUG", "0") == "1"
SAME_SYNC = os.environ.get("MK_SAMESYNC", "1") == "1"


class Tk:
    __slots__ = ("sem", "val", "eng")

    def __init__(self, sem, val, eng):
        self.sem, self.val, self.eng = sem, val, eng


class Buf:
    __slots__ = ("w", "r", "name")

    def __init__(self, name=""):
        self.w = None
        self.r = []
        self.name = name


class Eng:
    def __init__(self, ctx, h, name):
        self.ctx, self.h, self.name = ctx, h, name
        self.sem = ctx.es.enter_context(ctx.nc.semaphore("s_" + name))
        self.cnt = 0
        self.waited = {}
        self.pend = []

    def wait(self, tk):
        if tk is None:
            return
        if tk.eng is self and (self.name == "pe" or not SAME_SYNC):
            return
        if tk.val is None:
            raise RuntimeError("waiting on unsignalled ticket of " + tk.eng.name)
        key = id(tk.sem)
        if self.waited.get(key, 0) >= tk.val:
            return
        self.h.wait_ge(tk.sem, tk.val)
        self.waited[key] = tk.val

    def run(self, emit, reads=(), writes=(), signal=True):
        for b in reads:
            self.wait(b.w)
        for b in writes:
            self.wait(b.w)
            for t in b.r:
                self.wait(t)
        ins = emit()
        tk = Tk(self.sem, None, self)
        self.pend.append(tk)
        if signal:
            self.cnt += 1
            ins.then_inc(self.sem, 1)
            for p in self.pend:
                p.val = self.cnt
            self.pend = []
        for b in reads:
            add_read(b, tk)
        for b in writes:
            b.w = tk
            b.r = []
        return tk


def add_read(b, tk):
    if tk.eng is not None:
        b.r = [t for t in b.r if t.eng is not tk.eng]
    b.r.append(tk)


class DSem:
    def __init__(self, ctx, name):
        self.sem = ctx.es.enter_context(ctx.nc.semaphore("d_" + name))
        self.cnt = 0


class Ctx:
    pass


def dma(ctx, q, dsem, out, in_, reads=(), writes=()):
    for b in reads:
        q.wait(b.w)
    for b in writes:
        if not (b.w is not None and b.w.eng is None and b.w.sem is dsem.sem):
            q.wait(b.w)
        for t in b.r:
            q.wait(t)
    dsem.cnt += 16
    q.h.dma_start(out=out, in_=in_).then_inc(dsem.sem, 16)
    tk = Tk(dsem.sem, dsem.cnt, None)
    for b in reads:
        add_read(b, tk)
    for b in writes:
        b.w = tk
        b.r = []
    return tk


def barrier(ctx, dsems=()):
    engs = [ctx.pe, ctx.act, ctx.dve, ctx.pool, ctx.sp]
    tks = []
    for e in (ctx.pe, ctx.act, ctx.dve, ctx.pool):
        if e.pend:
            raise RuntimeError("pending tickets at barrier on " + e.name)
        if e.cnt > 0:
            tks.append(Tk(e.sem, e.cnt, e))
    for d in dsems:
        if d.cnt > 0:
            tks.append(Tk(d.sem, d.cnt, None))
    for e in engs:
        for t in tks:
            if t.eng is e:
                continue
            e.wait(t)


def build():
    nc = bass.Bass("TRN2", target_bir_lowering=False)
    ctx = Ctx()
    ctx.nc = nc
    es = ExitStack()
    ctx.es = es

    def din(name, shape, dt=F32):
        return nc.dram_tensor(name, shape, dt, kind="ExternalInput").ap()

    xin = din("xin", [NBLK, 128, 8 * BLK])
    csin = din("csin", [NBLK, 128, 2 * BLK])
    w1i_d = din("w1i", [128, 8, 2 * DFF])
    w1o_d = din("w1o", [128, NJ, D])
    w2i_d = din("w2i", [128, 8, 2 * DFF])
    w2o_d = din("w2o", [128, NJ, D])
    wkv_d = din("wkv", [128, 8, 1024])
    wqg_d = din("wqg", [128, 8, 1536])
    wmo_d = din("wmo", [128, 8, 1024])
    wada_d = din("wada", [128, 8, 9216])
    cvec_d = din("cvec", [128, NCV])
    cmat_d = din("cmat", [128, 3 * 128])
    out_d = nc.dram_tensor("out", [128, 8, TOK], F32, kind="ExternalOutput").ap()

    skind = "ExternalOutput" if DEBUG else "Internal"
    Kscr = nc.dram_tensor("kscr", [4, 128, S], BF16, kind=skind).ap()
    Vscr = nc.dram_tensor("vscr", [4, 128, 128, VW], BF16, kind=skind).ap()
    X1scr = nc.dram_tensor("x1scr", [OWNB, 128, 8 * BLK], F32, kind=skind).ap()
    X2scr = nc.dram_tensor("x2scr", [OWNB, 128, 8 * BLK], F32, kind=skind).ap()
    H2scr = nc.dram_tensor("h2scr", [128, 8, TOK + 2 * BLK], BF16, kind=skind).ap()
    W2Ib = nc.dram_tensor("w2ib", [128, 8, 2 * DFF], BF16, kind="Internal").ap()
    W2Ob = nc.dram_tensor("w2ob", [128, NJ, D], BF16, kind="Internal").ap()

    with es:
        block = es.enter_context(nc.Block())

        @block.sync
        def _(sync):
            program(ctx, locals_=dict(
                xin=xin, csin=csin, w1i_d=w1i_d, w1o_d=w1o_d, w2i_d=w2i_d, w2o_d=w2o_d, wkv_d=wkv_d,
                wqg_d=wqg_d, wmo_d=wmo_d, wada_d=wada_d, cvec_d=cvec_d, cmat_d=cmat_d, out_d=out_d,
                Kscr=Kscr, Vscr=Vscr, X1scr=X1scr, X2scr=X2scr, H2scr=H2scr, W2Ib=W2Ib, W2Ob=W2Ob))
    return nc


def program(ctx, locals_):
    nc = ctx.nc
    es = ctx.es
    g = locals_
    xin, csin = g["xin"], g["csin"]
    Kscr, Vscr, X1scr, X2scr, H2scr = g["Kscr"], g["Vscr"], g["X1scr"], g["X2scr"], g["H2scr"]
    out_d = g["out_d"]

    pe = ctx.pe = Eng(ctx, nc.tensor, "pe")
    act = ctx.act = Eng(ctx, nc.scalar, "act")
    dve = ctx.dve = Eng(ctx, nc.vector, "dve")
    pool = ctx.pool = Eng(ctx, nc.gpsimd, "pool")
    sp = ctx.sp = Eng(ctx, nc.sync, "sp")
    all_dsems = []

    def dsem(name):
        d = DSem(ctx, name)
        all_dsems.append(d)
        return d

    def sb(scope, name, shape, dt=F32):
        return scope.enter_context(nc.sbuf_tensor("t_" + name, shape, dt))

    psf = es.enter_context(nc.psum_tensor("psf", [128, 7 * 512], F32))
    psb32 = es.enter_context(nc.psum_tensor("psb", [128, 512], F32))
    psb = psb32[:].bitcast(BF16)

    def bank(i, w=512):
        return psf[:, i * 512:i * 512 + w]

    PB = [Buf("bank%d" % i) for i in range(8)]

    cvec = sb(es, "cvec", [128, NCV])
    cmatf = sb(es, "cmatf", [128, 384])
    cmb = sb(es, "cmb", [128, 384], BF16)
    modT = sb(es, "modT", [128, 72])
    dv = sb(es, "dv", [128, 80])
    cact = sb(es, "cact", [128, 8])
    lamt = sb(es, "lamt", [128, 8])
    gsub = sb(es, "gsub", [128, 128])
    B_c = Buf("consts")

    ident_b = cmb[:, 0:128]
    perm_b = cmb[:, 128:256]
    ones_b = cmb[:, 256:384]
    ones_f = cmatf[:, 256:384]

    d_c = dsem("c")
    dma(ctx, sp, d_c, cvec[:], g["cvec_d"][:, :], writes=[B_c])
    dma(ctx, sp, d_c, cmatf[:], g["cmat_d"][:, :], writes=[B_c])
    B_cmb = Buf("cmb")
    dve.run(lambda: nc.vector.tensor_copy(out=cmb[:], in_=cmatf[:]), reads=[B_c], writes=[B_cmb])

    V_SC0, V_SH0, V_GH0, V_G2, V_B2, V_GM, V_SC2, V_SH2, V_GH2 = [i * 8 for i in range(9)]

    def dvc(base, m):
        return dv[:, base + m:base + m + 1]

    def cvc(col):
        return cvec[:, col:col + 1]

    I32 = mybir.dt.int32

    def dve_rsqrt_ops(out, v, y, t, eps, scale, bufs):
        def r(fn):
            return lambda: dve.run(fn, reads=bufs, writes=bufs)
        ops = [r(lambda: V.tensor_scalar(out=out, in0=v, scalar1=scale, scalar2=eps, op0=ALU.mult, op1=ALU.add)),
               r(lambda: V.tensor_scalar(out=y.bitcast(I32), in0=out.bitcast(I32), scalar1=1, scalar2=None,
                                         op0=ALU.logical_shift_right)),
               r(lambda: V.tensor_scalar(out=y.bitcast(I32), in0=y.bitcast(I32), scalar1=-1, scalar2=0x5f3759df,
                                         op0=ALU.mult, op1=ALU.add))]
        for it in range(3):
            dst = out if it == 2 else y
            ops.append(r(lambda: V.tensor_tensor(out=t, in0=y, in1=y, op=ALU.mult)))
            ops.append(r(lambda: V.tensor_tensor(out=t, in0=t, in1=out, op=ALU.mult)))
            ops.append(r(lambda: V.tensor_scalar(out=t, in0=t, scalar1=-0.5, scalar2=1.5, op0=ALU.mult, op1=ALU.add)))
            ops.append(r(lambda dst=dst: V.tensor_tensor(out=dst, in0=y, in1=t, op=ALU.mult)))
        return ops

    def dve_rsqrt(out, v, y, t, eps, scale, bufs):
        for f in dve_rsqrt_ops(out, v, y, t, eps, scale, bufs):
            f()

    B_mod = Buf("mod")
    sc0 = ExitStack()
    GW = 2304
    wa = [sb(sc0, "wa%d" % i, [128, 8, GW], BF16) for i in range(2)]
    cactb = sb(sc0, "cactb", [128, 8], BF16)
    B_wa = [Buf("wa0"), Buf("wa1")]
    d_wa = [dsem("wa0"), dsem("wa1")]
    B_cact = Buf("cact")
    act.run(lambda: nc.scalar.activation(out=cact[:], in_=cvec[:, C_C:C_C + 8], func=AF.Silu),
            reads=[B_c], writes=[B_cact])
    dve.run(lambda: nc.vector.tensor_copy(out=cactb[:], in_=cact[:]), reads=[B_cact], writes=[B_cact])
    for gi in range(9216 // GW):
        s = gi % 2
        dma(ctx, pool, d_wa[s], wa[s][:], g["wada_d"][:, :, gi * GW:(gi + 1) * GW], writes=[B_wa[s]])
        for jj in range(GW // 128):
            j = gi * (GW // 128) + jj
            for kc in range(8):
                last = (kc == 7)
                pe.run(lambda: nc.tensor.matmul(bank(0)[:, j:j + 1], lhsT=wa[s][:, kc, jj * 128:(jj + 1) * 128],
                                                rhs=cactb[:, kc:kc + 1], start=(kc == 0), stop=last),
                       reads=[B_wa[s], B_cact], writes=[PB[0]], signal=last)
    dve.run(lambda: nc.vector.tensor_tensor(out=modT[:], in0=bank(0)[:, 0:72], in1=cvec[:, C_BADA:C_BADA + 72],
                                            op=ALU.add), reads=[PB[0], B_c], writes=[B_mod])

    def mod(s, t):
        c = (s * 3 + t) * 8
        return modT[:, c:c + 8]

    def lnv(i):
        return cvec[:, C_LN + 8 * i:C_LN + 8 * i + 8]

    B_dv = Buf("dv")
    V = nc.vector

    def dvrun(fn):
        dve.run(fn, reads=[B_mod, B_c, B_dv], writes=[B_dv])

    dvrun(lambda: V.tensor_scalar(out=dv[:, V_SC0:V_SC0 + 8], in0=mod(0, 1), scalar1=1.0, scalar2=None, op0=ALU.add))
    dvrun(lambda: V.tensor_copy(out=dv[:, V_SH0:V_SH0 + 8], in_=mod(0, 0)))
    dvrun(lambda: V.tensor_scalar(out=dv[:, V_GH0:V_GH0 + 8], in0=mod(0, 2), scalar1=1.0, scalar2=0.5 / ALPHA,
                                  op0=ALU.add, op1=ALU.mult))
    dvrun(lambda: V.tensor_scalar(out=dv[:, V_B2:V_B2 + 8], in0=mod(1, 1), scalar1=1.0, scalar2=None, op0=ALU.add))
    dvrun(lambda: V.tensor_tensor(out=dv[:, V_G2:V_G2 + 8], in0=dv[:, V_B2:V_B2 + 8], in1=lnv(0), op=ALU.mult))
    dvrun(lambda: V.tensor_tensor(out=dv[:, V_B2:V_B2 + 8], in0=dv[:, V_B2:V_B2 + 8], in1=lnv(1), op=ALU.mult))
    dvrun(lambda: V.tensor_tensor(out=dv[:, V_B2:V_B2 + 8], in0=dv[:, V_B2:V_B2 + 8], in1=mod(1, 0), op=ALU.add))
    dvrun(lambda: V.tensor_scalar(out=dv[:, V_GM:V_GM + 8], in0=mod(1, 2), scalar1=1.0, scalar2=1.0 / ALPHA,
                                  op0=ALU.add, op1=ALU.mult))
    dvrun(lambda: V.tensor_scalar(out=dv[:, V_SC2:V_SC2 + 8], in0=mod(2, 1), scalar1=1.0, scalar2=None, op0=ALU.add))
    dvrun(lambda: V.tensor_copy(out=dv[:, V_SH2:V_SH2 + 8], in_=mod(2, 0)))
    dvrun(lambda: V.tensor_scalar(out=dv[:, V_GH2:V_GH2 + 8], in0=mod(2, 2), scalar1=1.0, scalar2=0.5 / ALPHA,
                                  op0=ALU.add, op1=ALU.mult))
    dvrun(lambda: V.tensor_scalar(out=gsub[:], in0=cvec[:, C_SUBG:C_SUBG + 128], scalar1=1.0 - LAM_INIT,
                                  scalar2=None, op0=ALU.mult))
    B_lam = Buf("lam")
    dve.run(lambda: V.tensor_tensor(out=lamt[:, 0:1], in0=cvec[:, C_LAMV:C_LAMV + 1], in1=cvec[:, C_LAMV + 1:C_LAMV + 2],
                                    op=ALU.mult), reads=[B_c], writes=[B_lam])
    dve.run(lambda: V.tensor_tensor(out=lamt[:, 1:2], in0=cvec[:, C_LAMV + 2:C_LAMV + 3], in1=cvec[:, C_LAMV + 3:C_LAMV + 4],
                                    op=ALU.mult), reads=[B_c, B_lam], writes=[B_lam])
    pe.run(lambda: nc.tensor.matmul(bank(1)[:, 0:2], lhsT=ones_f, rhs=lamt[:, 0:2], start=True, stop=True),
           reads=[B_lam, B_c], writes=[PB[1]])
    act.run(lambda: nc.scalar.activation(out=lamt[:, 2:4], in_=bank(1)[:, 0:2], func=AF.Exp),
            reads=[PB[1]], writes=[B_lam])
    dve.run(lambda: V.tensor_tensor(out=lamt[:, 4:5], in0=lamt[:, 2:3], in1=lamt[:, 3:4], op=ALU.subtract),
            reads=[B_lam], writes=[B_lam])
    dve.run(lambda: V.tensor_scalar(out=lamt[:, 5:6], in0=lamt[:, 4:5], scalar1=LAM_INIT, scalar2=None, op0=ALU.add),
            reads=[B_lam], writes=[B_lam])
    LAM = lamt[:, 5:6]
    barrier(ctx, all_dsems)
    sc0.close()

    def ffn_phase(tag, nblk, x_src, wi_d, wo_d, vsc, vsh, vgh, mode, wq=None, wsrc_bufs=()):
        sc = ExitStack()
        w_in = sb(sc, tag + "wi", [128, 8, 2 * DFF], BF16)
        w_out = sb(sc, tag + "wo", [128, NJ, D], BF16)
        xb = [sb(sc, tag + "xb%d" % i, [128, 8, BLK]) for i in range(2)]
        hb = [sb(sc, tag + "hb%d" % i, [128, 8, BLK], BF16) for i in range(2)]
        actt = sb(sc, tag + "act", [128, NJ, BLK], BF16)
        sg = [sb(sc, tag + "sg%d" % i, [128, BLK]) for i in range(2)]
        rb = [sb(sc, tag + "rb%d" % i, [128, BLK], BF16) for i in range(2)]
        sq = [sb(sc, tag + "sq%d" % i, [128, BLK], BF16) for i in range(2)]
        mu = sb(sc, tag + "mu", [128, BLK])
        rstd = sb(sc, tag + "rstd", [128, BLK])
        nmr = sb(sc, tag + "nmr", [128, BLK])
        t1 = sb(sc, tag + "t1", [128, BLK])
        t2 = sb(sc, tag + "t2", [128, BLK])
        B_t1, B_t2 = Buf("t1"), Buf("t2")
        B_win, B_wout = Buf("win"), Buf("wout")
        B_xb = [[Buf("xb%d_%d" % (i, m)) for m in range(8)] for i in range(2)]
        B_hb = [[Buf("hb%d_%d" % (i, m)) for m in range(8)] for i in range(2)]
        B_act = [Buf("act%d" % j) for j in range(NJ)]
        B_sg = [Buf("sg0"), Buf("sg1")]
        B_rb = [Buf("rb0"), Buf("rb1")]
        B_sq = [Buf("sq0"), Buf("sq1")]
        B_st = Buf("stats")
        d_wi, d_wo, d_wk = dsem(tag + "wi"), dsem(tag + "wo"), dsem(tag + "wk")
        d_x = [dsem(tag + "x0"), dsem(tag + "x1")]
        d_xs = [dsem(tag + "xs0"), dsem(tag + "xs1")]
        p1 = (mode == "p1")
        if p1:
            wkv = sb(sc, "wkv", [128, 8, 1024], BF16)
            cs = [sb(sc, "cs%d" % i, [128, 2, BLK]) for i in range(2)]
            kT = sb(sc, "kT", [128, 4, BLK], BF16)
            vout = sb(sc, "vout", [128, 2, 4, VW], BF16)
            kb = sb(sc, "kb", [128, BLK], BF16)
            B_wkv, B_cs = Buf("wkv"), [Buf("cs0"), Buf("cs1")]
            B_kT, B_vout, B_kb = Buf("kT"), Buf("vout"), Buf("kb")
            d_cs = [dsem("cs0"), dsem("cs1")]
            d_k, d_v = dsem("kst"), dsem("vst")
            d_hs = [dsem("hs0"), dsem("hs1")]
            B_scr = Buf("scr")
            dve.run(lambda: V.memset(vout[:], 1.0), writes=[B_vout])
        wq = wq or pool
        for kc in range(8):
            dma(ctx, wq, d_wi, w_in[:, kc, :], wi_d[:, kc, :], reads=list(wsrc_bufs), writes=[B_win])
        for c4 in range(2):
            dma(ctx, wq, d_wo, w_out[:, c4 * 11:(c4 + 1) * 11, :], wo_d[:, c4 * 11:(c4 + 1) * 11, :],
                reads=list(wsrc_bufs), writes=[B_wout])
        if p1:
            dma(ctx, pool, d_wk, wkv[:], g["wkv_d"][:, :, :], writes=[B_wkv])

        def load_x(b):
            s = b % 2
            dma(ctx, sp, d_x[s], xb[s][:].rearrange("p a t -> p (a t)"), x_src[b], writes=B_xb[s])

        def load_cs(b):
            s = b % 2
            dma(ctx, sp, d_cs[s], cs[s][:].rearrange("p a t -> p (a t)"), csin[b], writes=[B_cs[s]])

        def modulate(b):
            s = b % 2
            for kc in range(8):
                act.run(lambda: nc.scalar.activation(out=hb[s][:, kc, :], in_=xb[s][:, kc, :], func=AF.Identity,
                                                     scale=dvc(vsc, kc), bias=dvc(vsh, kc)),
                        reads=[B_xb[s][kc], B_dv], writes=[B_hb[s][kc]])

        def kv_pieces(b):
            s = b % 2
            pieces = []

            def kpieceA(hc):
                bk = 4 + hc % 2
                def f():
                    for kc in range(8):
                        pe.run(lambda: nc.tensor.matmul(bank(bk)[:, 0:BLK], lhsT=wkv[:, kc, hc * 128:(hc + 1) * 128],
                                                        rhs=hb[s][:, kc, :], start=(kc == 0), stop=(kc == 7)),
                               reads=[B_wkv, B_hb[s][kc]], writes=[PB[bk]], signal=(kc == 7))
                    dve.run(lambda: V.tensor_tensor(out=t1[:], in0=bank(bk)[:, 0:BLK], in1=cs[s][:, 0, :], op=ALU.mult),
                            reads=[PB[bk], B_cs[s]], writes=[B_t1])
                    dve.run(lambda: V.tensor_copy(out=kb[:], in_=bank(bk)[:, 0:BLK]), reads=[PB[bk]], writes=[B_kb])
                return f

            def kpieceB(hc):
                def f():
                    pe.run(lambda: nc.tensor.matmul(bank(6)[:, 0:BLK], lhsT=perm_b, rhs=kb[:], start=True, stop=True),
                           reads=[B_kb, B_cmb], writes=[PB[6]])
                    dve.run(lambda: V.tensor_tensor(out=t2[:], in0=bank(6)[:, 0:BLK], in1=cs[s][:, 1, :], op=ALU.mult),
                            reads=[PB[6], B_cs[s]], writes=[B_t2])
                    dve.run(lambda: V.tensor_tensor(out=kT[:, hc, :], in0=t1[:], in1=t2[:], op=ALU.add),
                            reads=[B_t1, B_t2], writes=[B_kT])
                return f

            def vpiece(tt):
                bk = 4 + tt
                def f():
                    for kc in range(8):
                        pe.run(lambda: nc.tensor.matmul(bank(bk)[:, 0:512], lhsT=hb[s][:, kc, tt * 128:(tt + 1) * 128],
                                                        rhs=wkv[:, kc, 512:1024], start=(kc == 0), stop=(kc == 7)),
                               reads=[B_wkv, B_hb[s][kc]], writes=[PB[bk]], signal=(kc == 7))
                    act.run(lambda: nc.scalar.activation(out=vout[:, tt, :, 0:128],
                                                         in_=bank(bk).rearrange("p (h d) -> p h d", h=4),
                                                         func=AF.Copy), reads=[PB[bk]], writes=[B_vout])
                return f

            def store():
                dma(ctx, sp, d_k, Kscr[:, :, b * BLK:(b + 1) * BLK].rearrange("h p t -> p h t"), kT[:],
                    reads=[B_kT], writes=[B_scr])
                for tt in range(2):
                    dma(ctx, sp, d_v, Vscr[:, :, 2 * b + tt, :].rearrange("h p d -> p h d"), vout[:, tt, :, :],
                        reads=[B_vout], writes=[B_scr])

            for hc in range(4):
                pieces.append(kpieceA(hc))
                pieces.append(kpieceB(hc))
            for tt in range(2):
                pieces.append(vpiece(tt))
            pieces.append(store)
            return pieces

        def stage1(b, pieces):
            s = b % 2
            npc = len(pieces)
            if npc <= 11:
                slots = list(range(0, 22, 2))
            else:
                slots = list(range(NJ))
            assert npc <= len(slots), npc
            sched = {slots[i]: i for i in range(npc)}
            for j in range(NJ):
                ss = j % 2
                for half in range(2):
                    cb = half * DFF + j * 128
                    bk = 2 * ss + half
                    for kc in range(8):
                        pe.run(lambda: nc.tensor.matmul(bank(bk)[:, 0:BLK], lhsT=w_in[:, kc, cb:cb + 128],
                                                        rhs=hb[s][:, kc, :], start=(kc == 0), stop=(kc == 7)),
                               reads=[B_win, B_hb[s][kc]], writes=[PB[bk]], signal=(kc == 7))
                act.run(lambda: nc.scalar.activation(out=sg[ss][:], in_=bank(2 * ss)[:, 0:BLK], func=AF.Silu),
                        reads=[PB[2 * ss]], writes=[B_sg[ss]])
                dve.run(lambda: V.tensor_tensor(out=actt[:, j, :], in0=bank(2 * ss + 1)[:, 0:BLK], in1=sg[ss][:],
                                                op=ALU.mult), reads=[PB[2 * ss + 1], B_sg[ss]], writes=[B_act[j]])
                if j in sched:
                    pieces[sched[j]]()

        def stage2(b):
            s = b % 2

            def stats_mm(m):
                r2 = m % 2
                pe.run(lambda: nc.tensor.matmul(bank(6)[:, 0:BLK], lhsT=ones_b, rhs=rb[r2][:], start=(m == 0), stop=(m == 7),
                                                skip_group_check=True), reads=[B_rb[r2], B_cmb], writes=[PB[6]], signal=False)
                pe.run(lambda: nc.tensor.matmul(bank(6)[:, BLK:2 * BLK], lhsT=ones_b, rhs=sq[r2][:], start=False, stop=(m == 7),
                                                skip_group_check=True), reads=[B_sq[r2]], writes=[PB[6]], signal=True)

            for m in range(8):
                bk = 4 + (m % 2)
                for kc in range(NJ):
                    pe.run(lambda: nc.tensor.matmul(bank(bk)[:, 0:BLK], lhsT=w_out[:, kc, m * 128:(m + 1) * 128],
                                                    rhs=actt[:, kc, :], start=(kc == 0), stop=(kc == NJ - 1)),
                           reads=[B_wout, B_act[kc]], writes=[PB[bk]], signal=(kc == NJ - 1))
                if m >= 1:
                    stats_mm(m - 1)
                dve.run(lambda: V.scalar_tensor_tensor(out=xb[s][:, m, :], in0=bank(bk)[:, 0:BLK], scalar=dvc(vgh, m),
                                                       in1=xb[s][:, m, :], op0=ALU.mult, op1=ALU.add),
                        reads=[PB[bk], B_dv], writes=[B_xb[s][m]])
                r2 = m % 2
                pool.run(lambda: nc.gpsimd.tensor_copy(out=rb[r2][:], in_=xb[s][:, m, :]),
                         reads=[B_xb[s][m]], writes=[B_rb[r2]])
                pool.run(lambda: nc.gpsimd.tensor_tensor(out=sq[r2][:], in0=xb[s][:, m, :], in1=xb[s][:, m, :], op=ALU.mult),
                         reads=[B_xb[s][m]], writes=[B_sq[r2]])
            return lambda: stats_mm(7)

        def tail_pieces(b):
            s = b % 2
            inv = 1.0 / float(D)
            pcs = []

            ops = [
                lambda: dve.run(lambda: V.tensor_scalar(out=mu[:], in0=bank(6)[:, 0:BLK], scalar1=inv, scalar2=None,
                                                        op0=ALU.mult), reads=[PB[6]], writes=[B_st]),
                lambda: dve.run(lambda: V.tensor_tensor(out=nmr[:], in0=mu[:], in1=mu[:], op=ALU.mult),
                                reads=[B_st], writes=[B_st]),
                lambda: dve.run(lambda: V.scalar_tensor_tensor(out=rstd[:], in0=bank(6)[:, BLK:2 * BLK], scalar=inv, in1=nmr[:],
                                                               op0=ALU.mult, op1=ALU.subtract),
                                reads=[PB[6], B_st], writes=[B_st]),
            ]
            ops += dve_rsqrt_ops(rstd[:], rstd[:], t1[:], t2[:], EPS_A, 1.0, [B_st, B_t1, B_t2])
            ops.append(lambda: dve.run(lambda: V.scalar_tensor_tensor(out=nmr[:], in0=mu[:], scalar=-1.0, in1=rstd[:],
                                                                      op0=ALU.mult, op1=ALU.mult),
                                       reads=[B_st], writes=[B_st]))
            for i0 in range(0, len(ops), 4):
                def grp(i0=i0):
                    for f in ops[i0:i0 + 4]:
                        f()
                pcs.append(grp)

            def t_norm(lo, hi):
                def f():
                    sl = slice(lo, hi)
                    n = hi - lo
                    bufs = B_xb[s][lo:hi]
                    dve.run(lambda: V.tensor_tensor(out=xb[s][:, sl, :], in0=xb[s][:, sl, :],
                                                    in1=rstd[:].unsqueeze(1).broadcast_to([128, n, BLK]), op=ALU.mult),
                            reads=[B_st] + bufs, writes=bufs)
                    dve.run(lambda: V.tensor_tensor(out=xb[s][:, sl, :], in0=xb[s][:, sl, :],
                                                    in1=nmr[:].unsqueeze(1).broadcast_to([128, n, BLK]), op=ALU.add),
                            reads=[B_st] + bufs, writes=bufs)
                return f
            pcs.append(t_norm(0, 3))
            pcs.append(t_norm(3, 6))
            pcs.append(t_norm(6, 8))

            def t_aff(h0, sc_fn, bi_fn, to_hb):
                def f():
                    for m in range(4 * h0, 4 * h0 + 4):
                        if to_hb and m % 2 == 1:
                            pool.run(lambda: nc.gpsimd.tensor_scalar(out=hb[s][:, m, :], in0=xb[s][:, m, :], scalar1=sc_fn(m),
                                                                     scalar2=bi_fn(m), op0=ALU.mult, op1=ALU.add),
                                     reads=[B_xb[s][m], B_dv, B_c], writes=[B_hb[s][m]])
                        elif to_hb:
                            act.run(lambda: nc.scalar.activation(out=hb[s][:, m, :], in_=xb[s][:, m, :], func=AF.Identity,
                                                                 scale=sc_fn(m), bias=bi_fn(m)),
                                    reads=[B_xb[s][m], B_dv, B_c], writes=[B_hb[s][m]])
                        else:
                            act.run(lambda: nc.scalar.activation(out=xb[s][:, m, :], in_=xb[s][:, m, :], func=AF.Identity,
                                                                 scale=sc_fn(m), bias=bi_fn(m)),
                                    reads=[B_xb[s][m], B_dv, B_c], writes=[B_xb[s][m]])
                return f

            if p1:
                pcs.append(t_aff(0, lambda m: dvc(V_G2, m), lambda m: dvc(V_B2, m), True))
                pcs.append(t_aff(1, lambda m: dvc(V_G2, m), lambda m: dvc(V_B2, m), True))
                col = None
                if b <= OWNB:
                    col = b * BLK
                elif b == NBLK - 1:
                    col = TOK + BLK
                own = b < OWNB

                def t_store():
                    if col is not None:
                        dma(ctx, sp, d_hs[s], H2scr[:, :, col:col + BLK], hb[s][:], reads=B_hb[s], writes=[B_scr])
                    if own:
                        t_aff(0, lambda m: cvc(C_LN + m), lambda m: cvc(C_LN + 8 + m), False)()
                        t_aff(1, lambda m: cvc(C_LN + m), lambda m: cvc(C_LN + 8 + m), False)()
                        dma(ctx, sp, d_xs[s], X1scr[b], xb[s][:].rearrange("p a t -> p (a t)"), reads=B_xb[s], writes=[B_scr])
                if col is not None or own:
                    pcs.append(t_store)
                pcs.extend(kv_pieces(b))
            else:
                pcs.append(t_aff(0, lambda m: cvc(C_LN + 32 + m), lambda m: cvc(C_LN + 40 + m), False))
                pcs.append(t_aff(1, lambda m: cvc(C_LN + 32 + m), lambda m: cvc(C_LN + 40 + m), False))

                def t_out():
                    dma(ctx, sp, d_xs[s], out_d[:, :, b * BLK:(b + 1) * BLK], xb[s][:], reads=B_xb[s], writes=[B_out])
                pcs.append(t_out)
            return pcs

        load_x(0)
        if p1:
            load_cs(0)
        modulate(0)
        pend = []
        for b in range(nblk):
            stage1(b, pend)
            if b + 1 < nblk:
                load_x(b + 1)
                if p1:
                    load_cs(b + 1)
                modulate(b + 1)
            last_stats = stage2(b)
            tp = tail_pieces(b)
            first = tp[0]
            pend = [(lambda ls=last_stats, f0=first: (ls(), f0()))] + tp[1:]
        for f in pend:
            f()
        barrier(ctx, all_dsems)
        sc.close()

    B_out = Buf("out")
    ffn_phase("a", NBLK, xin, g["w1i_d"], g["w1o_d"], V_SC0, V_SH0, V_GH0, "p1")

    scm = ExitStack()
    qT = sb(scm, "qT", [128, 4, TOK], BF16)
    attnT = sb(scm, "attnT", [128, 4, TOK], BF16)
    convT = sb(scm, "convT", [128, 4, TOK], BF16)
    uext = sb(scm, "uext", [128, 4, UW])
    acc = sb(scm, "acc", [128, 4, TOK])
    B_qT, B_attnT, B_convT = Buf("qT"), Buf("attnT"), Buf("convT")
    B_u = [Buf("u%d" % c) for c in range(4)]
    B_acc = [Buf("acc%d" % c) for c in range(4)]

    sc1 = ExitStack()
    wqg = sb(sc1, "wqg", [128, 8, 1536], BF16)
    h2t = [sb(sc1, "h2t%d" % i, [128, 8, 512], BF16) for i in range(2)]
    hal = sb(sc1, "hal", [128, 8, 2 * HALO], BF16)
    csq = [sb(sc1, "csq%d" % i, [128, 2, 512]) for i in range(2)]
    sgl = sb(sc1, "sgl", [128, 512])
    q1 = sb(sc1, "q1", [128, 512])
    q2 = sb(sc1, "q2", [128, 512])
    qb_ = sb(sc1, "qb", [128, 512], BF16)
    B_wqg, B_h2t, B_hal = Buf("wqg"), [Buf("h2t0"), Buf("h2t1")], Buf("hal")
    B_csq = [Buf("csq0"), Buf("csq1")]
    B_sgl, B_q1, B_q2, B_qb = Buf("sgl"), Buf("q1"), Buf("q2"), Buf("qb")
    d_wq, d_hal = dsem("wqg"), dsem("hal")
    d_h2 = [dsem("h2t0"), dsem("h2t1")]
    d_csq = [dsem("csq0"), dsem("csq1")]
    dma(ctx, pool, d_wq, wqg[:], g["wqg_d"][:, :, :], writes=[B_wqg])
    dma(ctx, sp, d_hal, hal[:, :, 0:HALO], H2scr[:, :, TOK + 2 * BLK - HALO:TOK + 2 * BLK], writes=[B_hal])
    dma(ctx, sp, d_hal, hal[:, :, HALO:2 * HALO], H2scr[:, :, TOK:TOK + HALO], writes=[B_hal])

    def glu_chunk(cc, rhs_fn, n, ucol, rbufs, bk, hm=None):
        for half in range(2):
            for kc in range(8):
                pe.run(lambda: nc.tensor.matmul(bank(bk + half)[:, 0:n], lhsT=wqg[:, kc, half * 512 + cc * 128:half * 512 + (cc + 1) * 128],
                                                rhs=rhs_fn(kc), start=(kc == 0), stop=(kc == 7)),
                       reads=[B_wqg] + rbufs, writes=[PB[bk + half]], signal=(kc == 7))
        act.run(lambda: nc.scalar.activation(out=sgl[:, 0:n], in_=bank(bk + 1)[:, 0:n], func=AF.Sigmoid),
                reads=[PB[bk + 1]], writes=[B_sgl])
        dve.run(lambda: V.tensor_tensor(out=uext[:, cc, ucol:ucol + n], in0=bank(bk)[:, 0:n], in1=sgl[:, 0:n], op=ALU.mult),
                reads=[PB[bk], B_sgl], writes=[B_u[cc]])
        if hm is not None:
            dve.run(lambda: V.tensor_scalar(out=uext[:, cc, ucol:ucol + n], in0=uext[:, cc, ucol:ucol + n],
                                            scalar1=cvc(C_HMASK + hm), scalar2=None, op0=ALU.mult),
                    reads=[B_c, B_u[cc]], writes=[B_u[cc]])

    for cc in range(4):
        glu_chunk(cc, lambda kc: hal[:, kc, 0:HALO], HALO, 0, [B_hal], 0, hm=0)
        glu_chunk(cc, lambda kc: hal[:, kc, HALO:2 * HALO], HALO, HALO + TOK, [B_hal], 2, hm=1)
    for tb in range(4):
        s = tb % 2
        dma(ctx, sp, d_h2[s], h2t[s][:], H2scr[:, :, tb * 512:(tb + 1) * 512], writes=[B_h2t[s]])
        for i in range(2):
            dma(ctx, sp, d_csq[s], csq[s][:, :, i * BLK:(i + 1) * BLK],
                csin[2 * tb + i].rearrange("p (a t) -> p a t", a=2), writes=[B_csq[s]])
        for cc in range(4):
            glu_chunk(cc, lambda kc: h2t[s][:, kc, :], 512, HALO + tb * 512, [B_h2t[s]], 2 * (cc % 2))
        for hc in range(4):
            for kc in range(8):
                pe.run(lambda: nc.tensor.matmul(bank(4)[:, 0:512], lhsT=wqg[:, kc, 1024 + hc * 128:1024 + (hc + 1) * 128],
                                                rhs=h2t[s][:, kc, :], start=(kc == 0), stop=(kc == 7)),
                       reads=[B_wqg, B_h2t[s]], writes=[PB[4]], signal=(kc == 7))
            dve.run(lambda: V.tensor_tensor(out=q1[:], in0=bank(4)[:, 0:512], in1=csq[s][:, 0, :], op=ALU.mult),
                    reads=[PB[4], B_csq[s]], writes=[B_q1])
            dve.run(lambda: V.tensor_copy(out=qb_[:], in_=bank(4)[:, 0:512]), reads=[PB[4]], writes=[B_qb])
            pe.run(lambda: nc.tensor.matmul(bank(5)[:, 0:512], lhsT=perm_b, rhs=qb_[:], start=True, stop=True),
                   reads=[B_qb, B_cmb], writes=[PB[5]])
            dve.run(lambda: V.tensor_tensor(out=q2[:], in0=bank(5)[:, 0:512], in1=csq[s][:, 1, :], op=ALU.mult),
                    reads=[PB[5], B_csq[s]], writes=[B_q2])
            dve.run(lambda: V.tensor_tensor(out=qT[:, hc, tb * 512:(tb + 1) * 512], in0=q1[:], in1=q2[:], op=ALU.add),
                    reads=[B_q1, B_q2], writes=[B_qT])
    barrier(ctx, all_dsems)
    sc1.close()

    def conv_ops():
        ops = []
        for cc in range(4):
            def first(cc=cc):
                dve.run(lambda: V.tensor_scalar(out=acc[:, cc, :], in0=uext[:, cc, 0:TOK], scalar1=cvc(C_CONVW + cc * 31),
                                                scalar2=cvc(C_CONVB + cc), op0=ALU.mult, op1=ALU.add),
                        reads=[B_u[cc], B_c], writes=[B_acc[cc]])
            ops.append(first)
            for tap in range(1, 31):
                def nxt(cc=cc, tap=tap):
                    dve.run(lambda: V.scalar_tensor_tensor(out=acc[:, cc, :], in0=uext[:, cc, tap:tap + TOK],
                                                           scalar=cvc(C_CONVW + cc * 31 + tap), in1=acc[:, cc, :],
                                                           op0=ALU.mult, op1=ALU.add),
                            reads=[B_u[cc], B_c, B_acc[cc]], writes=[B_acc[cc]])
                ops.append(nxt)
        return ops

    cops = conv_ops()

    sc2 = ExitStack()
    Kq = [sb(sc2, "Kq%d" % i, [128, 4096], BF16) for i in range(4)]
    Vq = [sb(sc2, "Vq%d" % i, [128, 32, VW], BF16) for i in range(4)]
    PT = [sb(sc2, "PT%d" % i, [128, 1024], BF16) for i in range(3)]
    osb = sb(sc2, "osb", [128, 3, 512])
    rz = sb(sc2, "rz", [128, 8])
    tt1 = sb(sc2, "tt1", [128, 128])
    junk = sb(sc2, "junk", [128, 128])
    ssq = sb(sc2, "ssq", [128, 4])
    ssq2 = sb(sc2, "ssq2", [128, 8])
    ssq3 = sb(sc2, "ssq3", [128, 4])
    B_ssq2 = Buf("ssq2")
    ao = sb(sc2, "ao", [128, 128], BF16)
    B_Kq = [Buf("Kq%d" % i) for i in range(4)]
    B_Vq = [Buf("Vq%d" % i) for i in range(4)]
    B_PT = [Buf("PT%d" % i) for i in range(3)]
    B_osb, B_rz, B_dd, B_tt1, B_junk, B_ssq, B_ao = [Buf(n) for n in ("osb", "rz", "dd", "tt1", "junk", "ssq", "ao")]
    d_K = [dsem("K%d" % i) for i in range(4)]
    d_V = [dsem("V%d" % i) for i in range(4)]
    B_PS = [Buf("pss0"), Buf("pss1")]
    B_PO = Buf("pso")
    B_PTR = Buf("psb")

    def pss(slot):
        return psf[:, slot * 1024:(slot + 1) * 1024]

    def pso(comp, qq):
        r = comp * 4 + qq
        return psf[:, (4 + r // 3) * 512 + (r % 3) * 129:(4 + r // 3) * 512 + (r % 3) * 129 + 129]

    def load_kv(h, qi):
        dma(ctx, sp, d_K[qi], Kq[qi][:], Kscr[h, :, qi * 4096:(qi + 1) * 4096], writes=[B_Kq[qi]])
        dma(ctx, sp, d_V[qi], Vq[qi][:], Vscr[h, :, qi * 32:(qi + 1) * 32, :], writes=[B_Vq[qi]])

    for qi in range(4):
        load_kv(0, qi)
    steps = [(h, qb, kc) for h in range(4) for qb in range(4) for kc in range(128)]

    def emit_S(t):
        h, qb, kc = steps[t]
        qi, kk = kc // 32, kc % 32
        sl = t % 2
        for comp in range(2):
            pe.run(lambda: nc.tensor.matmul(pss(sl)[:, comp * 512:(comp + 1) * 512],
                                            lhsT=Kq[qi][comp * 64:(comp + 1) * 64, kk * 128:(kk + 1) * 128],
                                            rhs=qT[comp * 64:(comp + 1) * 64, h, qb * 512:(qb + 1) * 512],
                                            start=True, stop=True),
                   reads=[B_Kq[qi], B_qT], writes=[B_PS[sl]], signal=(comp == 1))

    def emit_exp(t):
        sl, pl = t % 2, t % 3
        act.run(lambda: nc.scalar.activation(out=PT[pl][:], in_=pss(sl), func=AF.Exp, scale=SCALE),
                reads=[B_PS[sl]], writes=[B_PT[pl]])

    def emit_AV(t):
        h, qb, kc = steps[t]
        qi, kk = kc // 32, kc % 32
        pl = t % 3
        for comp in range(2):
            for qq in range(4):
                r = comp * 4 + qq
                pe.run(lambda: nc.tensor.matmul(pso(comp, qq), lhsT=PT[pl][:, comp * 512 + qq * 128:comp * 512 + (qq + 1) * 128],
                                                rhs=Vq[qi][:, kk, 0:129], start=(kc == 0 and r % 3 == 0),
                                                stop=(kc == 127), skip_group_check=True),
                       reads=[B_PT[pl], B_Vq[qi]], writes=[B_PO], signal=(r == 7))

    dd4 = sb(sc2, "dd4", [128, 4, 128])
    epi = []
    epi_t0 = 0

    def ocol(comp, qq, a, b):
        r = comp * 4 + qq
        return osb[:, r // 3, (r % 3) * 129 + a:(r % 3) * 129 + b]

    def epilogue_pieces(h, qb):
        pcs = []

        def e_rz():
            for comp in range(2):
                for qq in range(4):
                    dve.run(lambda: V.reciprocal(out=rz[:, comp * 4 + qq:comp * 4 + qq + 1], in_=ocol(comp, qq, 128, 129)),
                            reads=[B_osb], writes=[B_rz])
            dve.run(lambda: V.tensor_scalar(out=rz[:, 4:8], in0=rz[:, 4:8], scalar1=LAM, scalar2=None, op0=ALU.mult),
                    reads=[B_rz, B_lam], writes=[B_rz])
        pcs.append(e_rz)

        def e_dd(qq):
            def f():
                dve.run(lambda: V.tensor_scalar(out=tt1[:], in0=ocol(1, qq, 0, 128), scalar1=rz[:, 4 + qq:5 + qq], scalar2=None,
                                                op0=ALU.mult), reads=[B_osb, B_rz], writes=[B_tt1])
                dve.run(lambda: V.scalar_tensor_tensor(out=dd4[:, qq, :], in0=ocol(0, qq, 0, 128), scalar=rz[:, qq:qq + 1], in1=tt1[:],
                                                       op0=ALU.mult, op1=ALU.subtract), reads=[B_osb, B_rz, B_tt1], writes=[B_dd])
                dve.run(lambda: V.scalar_tensor_tensor(out=junk[:], in0=dd4[:, qq, :], scalar=1.0, in1=dd4[:, qq, :], op0=ALU.mult,
                                                       op1=ALU.mult, accum_out=ssq[:, qq:qq + 1]), reads=[B_dd], writes=[B_junk, B_ssq])
            return f
        for qq in range(4):
            pcs.append(e_dd(qq))

        def e_sqrt():
            dve_rsqrt(ssq2[:, 4:8], ssq[:, 0:4], ssq2[:, 0:4], ssq3[:, 0:4], EPS, 1.0 / 128.0, [B_ssq, B_ssq2])
        pcs.append(e_sqrt)

        def e_out(qq):
            def f():
                dve.run(lambda: V.scalar_tensor_tensor(out=ao[:], in0=dd4[:, qq, :], scalar=ssq2[:, 4 + qq:5 + qq], in1=gsub[:],
                                                       op0=ALU.mult, op1=ALU.mult), reads=[B_dd, B_ssq2, B_dv], writes=[B_ao])
                pe.run(lambda: nc.tensor.matmul(psb[:, qq * 128:(qq + 1) * 128], lhsT=ao[:], rhs=ident_b, is_transpose=True,
                                                start=True, stop=True), reads=[B_ao, B_cmb], writes=[B_PTR])
                dve.run(lambda: V.tensor_copy(out=attnT[:, h, qb * 512 + qq * 128:qb * 512 + (qq + 1) * 128],
                                              in_=psb[:, qq * 128:(qq + 1) * 128]), reads=[B_PTR], writes=[B_attnT])
            return f
        for qq in range(4):
            pcs.append(e_out(qq))
        return pcs

    B_w2pc = Buf("w2pc")
    d_pc = dsem("w2pc")
    for kc in range(8):
        dma(ctx, pool, d_pc, g["W2Ib"][:, kc, :], g["w2i_d"][:, kc, :], writes=[B_w2pc])
    for c4 in range(2):
        dma(ctx, pool, d_pc, g["W2Ob"][:, c4 * 11:(c4 + 1) * 11, :], g["w2o_d"][:, c4 * 11:(c4 + 1) * 11, :], writes=[B_w2pc])

    emit_S(0)
    emit_exp(0)
    emit_S(1)
    emit_exp(1)
    for t in range(len(steps)):
        h, qb, kc = steps[t]
        qi, kk = kc // 32, kc % 32
        if t + 2 < len(steps):
            emit_S(t + 2)
        emit_AV(t)
        if t + 2 < len(steps):
            emit_exp(t + 2)
        if cops and t % 3 == 0:
            cops.pop(0)()
        if qb == 3 and h < 3 and kk == 31:
            load_kv(h + 1, qi)
        if epi and (t - epi_t0) % 2 == 1:
            epi.pop(0)()
        if kc == 127:
            while epi:
                epi.pop(0)()
            for bk in range(3):
                dve.run(lambda: V.tensor_copy(out=osb[:, bk, :], in_=psf[:, (4 + bk) * 512:(5 + bk) * 512]),
                        reads=[B_PO], writes=[B_osb])
            epi.extend(epilogue_pieces(h, qb))
            epi_t0 = t
    while epi:
        epi.pop(0)()
    while cops:
        cops.pop(0)()
    barrier(ctx, all_dsems)
    sc2.close()

    sc4 = ExitStack()
    wmo = sb(sc4, "wmo", [128, 8, 1024], BF16)
    x1t = [sb(sc4, "x1t%d" % i, [128, 8, 512]) for i in range(2)]
    mu4 = sb(sc4, "mu4", [128, 512])
    rs4 = sb(sc4, "rs4", [128, 512])
    nm4 = sb(sc4, "nm4", [128, 512])
    B_wmo = Buf("wmo")
    B_x1t = [[Buf("x1t%d_%d" % (i, m)) for m in range(8)] for i in range(2)]
    B_st4 = Buf("st4")
    d_wm = dsem("wmo")
    d_x1 = [dsem("x1t0"), dsem("x1t1")]
    d_x2 = [dsem("x2s0"), dsem("x2s1")]
    B_scr2 = Buf("scr2")
    dma(ctx, pool, d_wm, wmo[:], g["wmo_d"][:, :, :], writes=[B_wmo])
    SB = [(5, 6), (2, 3)]

    rbb = [sb(sc4, "rbb%d" % i, [128, 512], BF16) for i in range(4)]
    sqb = [sb(sc4, "sqb%d" % i, [128, 512], BF16) for i in range(4)]
    B_rbb = [Buf("rbb%d" % i) for i in range(4)]
    B_sqb = [Buf("sqb%d" % i) for i in range(4)]
    stat_ctr = [0]

    def stat_mm(par, src, bufs, m, nch):
        b1, b2 = SB[par]
        r2 = stat_ctr[0] % 4
        stat_ctr[0] += 1
        act.run(lambda: nc.scalar.activation(out=rbb[r2][:], in_=src, func=AF.Copy), reads=bufs, writes=[B_rbb[r2]])
        act.run(lambda: nc.scalar.activation(out=sqb[r2][:], in_=src, func=AF.Square), reads=bufs, writes=[B_sqb[r2]])
        pe.run(lambda: nc.tensor.matmul(bank(b1), lhsT=ones_b, rhs=rbb[r2][:], start=(m == 0), stop=(m == nch - 1)),
               reads=[B_rbb[r2], B_cmb], writes=[PB[b1]], signal=(m == nch - 1))
        pe.run(lambda: nc.tensor.matmul(bank(b2), lhsT=ones_b, rhs=sqb[r2][:], start=(m == 0), stop=(m == nch - 1)),
               reads=[B_sqb[r2], B_cmb], writes=[PB[b2]], signal=True)

    def lnstats512(par, nfeat, eps):
        b1, b2 = SB[par]
        inv = 1.0 / nfeat
        dve.run(lambda: V.tensor_scalar(out=mu4[:], in0=bank(b1), scalar1=inv, scalar2=None, op0=ALU.mult),
                reads=[PB[b1]], writes=[B_st4])
        dve.run(lambda: V.tensor_tensor(out=nm4[:], in0=mu4[:], in1=mu4[:], op=ALU.mult), reads=[B_st4], writes=[B_st4])
        dve.run(lambda: V.scalar_tensor_tensor(out=rs4[:], in0=bank(b2), scalar=inv, in1=nm4[:], op0=ALU.mult, op1=ALU.subtract),
                reads=[PB[b2], B_st4], writes=[B_st4])
        act.run(lambda: nc.scalar.activation(out=rs4[:], in_=rs4[:], func=AF.Sqrt, bias=eps, scale=1.0),
                reads=[B_st4], writes=[B_st4])
        dve.run(lambda: V.reciprocal(out=rs4[:], in_=rs4[:]), reads=[B_st4], writes=[B_st4])
        dve.run(lambda: V.scalar_tensor_tensor(out=nm4[:], in0=mu4[:], scalar=-1.0, in1=rs4[:], op0=ALU.mult, op1=ALU.mult),
                reads=[B_st4], writes=[B_st4])

    B_acct = [[Buf("acct%d_%d" % (tb, cc)) for cc in range(4)] for tb in range(4)]

    def cl_A(tb):
        tsl = slice(tb * 512, (tb + 1) * 512)
        for cc in range(4):
            stat_mm(tb % 2, acc[:, cc, tsl], [B_acc[cc], B_acct[tb][cc]], cc, 4)

    def cl_B(tb):
        tsl = slice(tb * 512, (tb + 1) * 512)
        lnstats512(tb % 2, 512.0, EPS)
        dve.run(lambda: V.tensor_tensor(out=acc[:, :, tsl], in0=acc[:, :, tsl],
                                        in1=rs4[:].unsqueeze(1).broadcast_to([128, 4, 512]), op=ALU.mult),
                reads=[B_st4] + B_acct[tb], writes=B_acct[tb])
        dve.run(lambda: V.tensor_tensor(out=acc[:, :, tsl], in0=acc[:, :, tsl],
                                        in1=nm4[:].unsqueeze(1).broadcast_to([128, 4, 512]), op=ALU.add),
                reads=[B_st4] + B_acct[tb], writes=B_acct[tb])
        for cc in range(4):
            act.run(lambda: nc.scalar.activation(out=convT[:, cc, tsl], in_=acc[:, cc, tsl], func=AF.Silu,
                                                 scale=cvc(C_CLNG + cc), bias=cvc(C_CLNB + cc)),
                    reads=[B_acct[tb][cc], B_c], writes=[B_convT])

    cl_A(0)
    cl_A(1)
    cl_B(0)
    cl_A(2)
    cl_B(1)
    cl_A(3)
    cl_B(2)
    cl_B(3)

    def p4_tail(tb):
        s = tb % 2
        par = tb % 2
        pcs = []
        pcs.append(lambda: stat_mm(par, x1t[s][:, 7, :], [B_x1t[s][7]], 7, 8))
        pcs.append(lambda: lnstats512(par, float(D), EPS_A))

        def nrm(h0):
            def f():
                sl = slice(4 * h0, 4 * h0 + 4)
                bufs = B_x1t[s][4 * h0:4 * h0 + 4]
                dve.run(lambda: V.tensor_tensor(out=x1t[s][:, sl, :], in0=x1t[s][:, sl, :],
                                                in1=rs4[:].unsqueeze(1).broadcast_to([128, 4, 512]), op=ALU.mult),
                        reads=[B_st4] + bufs, writes=bufs)
                dve.run(lambda: V.tensor_tensor(out=x1t[s][:, sl, :], in0=x1t[s][:, sl, :],
                                                in1=nm4[:].unsqueeze(1).broadcast_to([128, 4, 512]), op=ALU.add),
                        reads=[B_st4] + bufs, writes=bufs)
                for m in range(4 * h0, 4 * h0 + 4):
                    act.run(lambda: nc.scalar.activation(out=x1t[s][:, m, :], in_=x1t[s][:, m, :], func=AF.Identity,
                                                         scale=cvc(C_LN + 16 + m), bias=cvc(C_LN + 24 + m)),
                            reads=[B_x1t[s][m], B_c], writes=[B_x1t[s][m]])
            return f
        pcs.append(nrm(0))
        pcs.append(nrm(1))

        def st():
            for i in range(2):
                dma(ctx, sp, d_x2[s], X2scr[2 * tb + i].rearrange("p (a t) -> p a t", a=8), x1t[s][:, :, i * BLK:(i + 1) * BLK],
                    reads=B_x1t[s], writes=[B_scr2])
        pcs.append(st)
        return pcs

    pend4 = []
    for tb in range(4):
        s = tb % 2
        par = tb % 2
        tsl = slice(tb * 512, (tb + 1) * 512)
        for m in range(8):
            bk = m % 2
            if m == 0:
                for i in range(2):
                    dma(ctx, sp, d_x1[s], x1t[s][:, :, i * BLK:(i + 1) * BLK],
                        X1scr[2 * tb + i].rearrange("p (a t) -> p a t", a=8), writes=B_x1t[s])
            for kc in range(8):
                rhs = convT[:, kc, tsl] if kc < 4 else attnT[:, kc - 4, tsl]
                pe.run(lambda: nc.tensor.matmul(bank(bk), lhsT=wmo[:, kc, m * 128:(m + 1) * 128], rhs=rhs,
                                                start=(kc == 0), stop=(kc == 7)),
                       reads=[B_wmo, B_convT, B_attnT], writes=[PB[bk]], signal=(kc == 7))
            if pend4:
                pend4.pop(0)()
            if m >= 1:
                stat_mm(par, x1t[s][:, m - 1, :], [B_x1t[s][m - 1]], m - 1, 8)
            dve.run(lambda: V.scalar_tensor_tensor(out=x1t[s][:, m, :], in0=bank(bk), scalar=dvc(V_GM, m), in1=x1t[s][:, m, :],
                                                   op0=ALU.mult, op1=ALU.add), reads=[PB[bk], B_dv], writes=[B_x1t[s][m]])
        while pend4:
            pend4.pop(0)()
        pend4 = p4_tail(tb)
    while pend4:
        pend4.pop(0)()
    barrier(ctx, all_dsems)
    sc4.close()
    scm.close()

    ffn_phase("b", OWNB, X2scr, g["W2Ib"], g["W2Ob"], V_SC2, V_SH2, V_GH2, "p5", wq=sp, wsrc_bufs=[B_w2pc])


def _pm(w, nk):
    return np.ascontiguousarray(w.reshape(nk, 128, -1).transpose(1, 0, 2))


def _fm(v, n):
    return np.ascontiguousarray(np.asarray(v, np.float32).reshape(n, 128).T)


_NC_CACHE = {}


def kernel(**inp):
    x = np.asarray(inp["x"], np.float32)[0]
    c = np.asarray(inp["c"], np.float32)[0]
    f = lambda k: np.asarray(inp[k], np.float32)[0]
    w_ada, b_ada = f("w_ada"), f("b_ada")
    mix_w_in = f("mix_w_in")

    shared = {
        "w1i": _pm(f("ffn1_w_in"), 8), "w1o": _pm(f("ffn1_w_out"), NJ),
        "w2i": _pm(f("ffn2_w_in"), 8), "w2o": _pm(f("ffn2_w_out"), NJ),
        "wkv": _pm(mix_w_in[:, 1536:2560], 8), "wqg": _pm(mix_w_in[:, 0:1536], 8),
        "wmo": _pm(f("mix_w_out"), 8), "wada": _pm(w_ada, 8),
    }
    cv = np.zeros((128, NCV), np.float32)
    cv[:, C_C:C_C + 8] = _fm(c, 8)
    cv[:, C_BADA:C_BADA + 72] = _fm(b_ada, 72)
    for i, k in enumerate(("ln1_g", "ln1_b", "ln2_g", "ln2_b", "ln3_g", "ln3_b")):
        cv[:, C_LN + 8 * i:C_LN + 8 * i + 8] = _fm(f(k), 8)
    cw = f("conv_w")
    cv[:, C_CONVW:C_CONVW + 124] = cw.T.reshape(4, 128, 31).transpose(1, 0, 2).reshape(128, 124)
    cv[:, C_CONVB:C_CONVB + 4] = _fm(f("conv_b"), 4)
    cv[:, C_CLNG:C_CLNG + 4] = _fm(f("conv_ln_g"), 4)
    cv[:, C_CLNB:C_CLNB + 4] = _fm(f("conv_ln_b"), 4)
    for i, k in enumerate(("lambda_q1", "lambda_k1", "lambda_q2", "lambda_k2")):
        cv[0:64, C_LAMV + i] = f(k)
    cv[:, C_SUBG:C_SUBG + 128] = f("subln_g")[None, :]

    cm = np.zeros((128, 3, 128), np.float32)
    cm[:, 0, :] = np.eye(128, dtype=np.float32)
    for m in range(128):
        d = m % 64
        if d < 8:
            cm[m + 8, 1, m] = 1.0
        elif d < 16:
            cm[m - 8, 1, m] = 1.0
    cm[:, 2, :] = 1.0
    cm = cm.reshape(128, 384)

    pos = np.arange(S, dtype=np.float32)
    inv_freq = (500000.0 ** (-np.arange(0, 16, 2, dtype=np.float32) / np.float32(16))).astype(np.float32)
    ang = (pos[:, None] * inv_freq[None, :]).astype(np.float32)
    cos, sin = np.cos(ang).astype(np.float32), np.sin(ang).astype(np.float32)
    Ct = np.ones((128, S), np.float32)
    St = np.zeros((128, S), np.float32)
    for comp in range(2):
        b0 = comp * 64
        Ct[b0:b0 + 8] = cos.T
        Ct[b0 + 8:b0 + 16] = cos.T
        St[b0:b0 + 8] = -sin.T
        St[b0 + 8:b0 + 16] = sin.T

    xT = np.ascontiguousarray(x.T)
    in_maps = []
    for r in range(NCORES):
        idx = (np.arange(S) + r * TOK) % S
        xr = xT[:, idx].reshape(8, 128, NBLK, BLK).transpose(2, 1, 0, 3).reshape(NBLK, 128, 8 * BLK)
        csr = np.stack([Ct[:, idx], St[:, idx]], 1).reshape(128, 2, NBLK, BLK).transpose(2, 0, 1, 3).reshape(NBLK, 128, 2 * BLK)
        cvr = cv.copy()
        cvr[:, C_HMASK] = 1.0 if r > 0 else 0.0
        cvr[:, C_HMASK + 1] = 1.0 if r < NCORES - 1 else 0.0
        m = dict(shared)
        m.update({"xin": np.ascontiguousarray(xr), "csin": np.ascontiguousarray(csr), "cvec": cvr, "cmat": cm})
        in_maps.append(m)

    if "nc" not in _NC_CACHE:
        _NC_CACHE["nc"] = build()
    res = run_bass_kernel_spmd(_NC_CACHE["nc"], in_maps, core_ids=list(range(NCORES)))
    out = np.empty((1, S, D), np.float32)
    for r in range(NCORES):
        o = np.asarray(res.results[r]["out"])
        out[0, r * TOK:(r + 1) * TOK, :] = o.transpose(2, 1, 0).reshape(TOK, D)
    if DEBUG:
        kernel.debug = res.results
    return out
```
